# Optimizing a Trainium2 kernel written in Bass

```python
import jax, jax.numpy as jnp
from jax import lax
import numpy as np

D_MODEL = 1024
BATCH = 8
SEQ = 2048
DEPTH = 1
DEC_BATCH = 32
DEC_SEQ = 16
PAST_LEN = 1024

CHUNK = 64
D_CONV = D_MODEL // 2
D_ATTN = D_MODEL - D_CONV
N_HEADS = 8
HEAD_DIM = D_ATTN // N_HEADS
CONV_WIDTH = 31
D_FF = -(-8 * D_MODEL // (3 * 256)) * 256
IN_COLS = 2 * D_CONV + 3 * D_ATTN + N_HEADS
Q_BLOCK = 128
EPS = 1e-6
NEG_INF = -1e30

kernel_name = 'hybrid_conformer_conv_fox_adaln_step'


def _rmsnorm(x, g):
    x32 = x.astype(jnp.float32)
    y = x32 * lax.rsqrt(jnp.mean(x32 * x32, axis=-1, keepdims=True) + EPS)
    return (y * g.astype(jnp.float32)).astype(x.dtype)


def _modulation(c, w_ada, b_ada):
    mod = jax.nn.silu(c) @ w_ada + b_ada
    return jnp.split(mod[:, None, :], 6, axis=-1)


def _mixer_inputs(x, shift, scale, norm1_g, w_in, b_f, q_norm_g, k_norm_g):
    b, t = x.shape[0], x.shape[1]
    h = _rmsnorm(x, norm1_g) * (1 + scale) + shift
    z = h @ w_in
    u_a, u_g, q, k, v, f = jnp.split(
        z, [D_CONV, 2 * D_CONV, 2 * D_CONV + D_ATTN, 2 * D_CONV + 2 * D_ATTN,
            2 * D_CONV + 3 * D_ATTN], axis=-1)
    u = u_a * jax.nn.sigmoid(u_g)
    q = _rmsnorm(q.reshape(b, t, N_HEADS, HEAD_DIM), q_norm_g)
    k = _rmsnorm(k.reshape(b, t, N_HEADS, HEAD_DIM), k_norm_g)
    v = v.reshape(b, t, N_HEADS, HEAD_DIM)
    logf = jax.nn.log_sigmoid((f + b_f).astype(jnp.float32))
    return u, q, k, v, logf


def _conv_module(u_hist, conv_w, conv_b, ln_g, ln_b):
    y = lax.conv_general_dilated(
        u_hist, conv_w[:, None, :], window_strides=(1,), padding='VALID',
        dimension_numbers=('NWC', 'WIO', 'NWC'), feature_group_count=D_CONV) + conv_b
    y32 = y.astype(jnp.float32)
    mu = jnp.mean(y32, axis=-1, keepdims=True)
    var = jnp.mean(jnp.square(y32 - mu), axis=-1, keepdims=True)
    y32 = (y32 - mu) * lax.rsqrt(var + EPS) * ln_g.astype(jnp.float32) + ln_b.astype(jnp.float32)
    return jax.nn.silu(y32).astype(u_hist.dtype)


def _attend(q, k, v, cum_q, cum_k, q_pos, k_pos):
    s = jnp.einsum('bqhd,bkhd->bhqk', q, k, preferred_element_type=jnp.float32) * (HEAD_DIM ** -0.5)
    s = s + (cum_q[..., :, None] - cum_k[..., None, :])
    mask = k_pos[None, :] <= q_pos[:, None]
    s = jnp.where(mask, s, NEG_INF)
    p = jax.nn.softmax(s, axis=-1)
    return jnp.einsum('bhqk,bkhd->bqhd', p.astype(v.dtype), v)


def _prompt_attention(q, k, v, logf):
    b, s = q.shape[0], q.shape[1]
    nb = s // Q_BLOCK
    cum = jnp.cumsum(logf, axis=1).transpose(0, 2, 1)
    qb = q.reshape(b, nb, Q_BLOCK, N_HEADS, HEAD_DIM).transpose(1, 0, 2, 3, 4)
    cb = cum.reshape(b, N_HEADS, nb, Q_BLOCK).transpose(2, 0, 1, 3)
    pb = jnp.arange(s).reshape(nb, Q_BLOCK)
    k_pos = jnp.arange(s)
    out = lax.map(lambda a: _attend(a[0], k, v, a[1], cum, a[2], k_pos), (qb, cb, pb))
    return out.transpose(1, 0, 2, 3, 4).reshape(b, s, D_ATTN)


def _sample_attention(q, k, v, logf, ck, cv, clf):
    b, t = q.shape[0], q.shape[1]
    p_len = ck.shape[1]
    k_all = jnp.concatenate([ck, k], axis=1)
    v_all = jnp.concatenate([cv, v], axis=1)
    lf_all = jnp.concatenate([clf.astype(jnp.float32), logf], axis=1)
    cum = jnp.cumsum(lf_all, axis=1).transpose(0, 2, 1)
    q_pos = p_len + jnp.arange(t)
    k_pos = jnp.arange(p_len + t)
    out = _attend(q, k_all, v_all, cum[:, :, p_len:], cum, q_pos, k_pos)
    return out.reshape(b, t, D_ATTN)


def _layer(x, c, past, w_ada, b_ada, norm1_g, w_in, b_f, q_norm_g, k_norm_g, conv_w, conv_b,
           conv_ln_g, conv_ln_b, w_out, norm2_g, w_gate, w_up, w_down):
    sh1, sc1, g1, sh2, sc2, g2 = _modulation(c, w_ada, b_ada)
    u, q, k, v, logf = _mixer_inputs(x, sh1, sc1, norm1_g, w_in, b_f, q_norm_g, k_norm_g)
    if past is None:
        u_hist = jnp.pad(u, ((0, 0), (CONV_WIDTH - 1, 0), (0, 0)))
        attn = _prompt_attention(q, k, v, logf)
    else:
        conv_state, ck, cv, clf = past
        u_hist = jnp.concatenate([conv_state.astype(u.dtype), u], axis=1)
        attn = _sample_attention(q, k, v, logf, ck, cv, clf)
    conv_out = _conv_module(u_hist, conv_w, conv_b, conv_ln_g, conv_ln_b)
    new_conv = u_hist[:, u_hist.shape[1] - (CONV_WIDTH - 1):]
    mix = jnp.concatenate([conv_out, attn.astype(conv_out.dtype)], axis=-1) @ w_out
    x = x + g1 * mix
    h = _rmsnorm(x, norm2_g) * (1 + sc2) + sh2
    x = x + g2 * ((jax.nn.silu(h @ w_gate) * (h @ w_up)) @ w_down)
    return x, (k, v, logf, new_conv)


def setup_inputs(seed: int = 0) -> dict:
    key = jax.random.key(seed)
    ks = jax.random.split(key, 24)

    def nrm(k, shape, scale):
        return jax.random.normal(k, shape, jnp.float32) * scale

    L = DEPTH
    return {
        'x_prompt': nrm(ks[0], (BATCH, SEQ, D_MODEL), 1.0),
        'x_sample': nrm(ks[1], (DEC_BATCH, DEC_SEQ, D_MODEL), 1.0),
        'cache_k': nrm(ks[2], (L, DEC_BATCH, PAST_LEN, N_HEADS, HEAD_DIM), 1.0),
        'cache_v': nrm(ks[3], (L, DEC_BATCH, PAST_LEN, N_HEADS, HEAD_DIM), 1.0),
        'cache_logf': jax.nn.log_sigmoid(2.0 + nrm(ks[4], (L, DEC_BATCH, PAST_LEN, N_HEADS), 1.0)),
        'state_conv': nrm(ks[5], (L, DEC_BATCH, CONV_WIDTH - 1, D_CONV), 0.5),
        'c_prompt': nrm(ks[6], (BATCH, D_MODEL), 1.0),
        'c_sample': nrm(ks[7], (DEC_BATCH, D_MODEL), 1.0),
        'w_ada': nrm(ks[8], (L, D_MODEL, 6 * D_MODEL), D_MODEL ** -0.5),
        'b_ada': nrm(ks[9], (L, 6 * D_MODEL), 0.02),
        'norm1_g': 1.0 + nrm(ks[10], (L, D_MODEL), 0.02),
        'w_in': nrm(ks[11], (L, D_MODEL, IN_COLS), D_MODEL ** -0.5),
        'b_f': 2.0 + nrm(ks[12], (L, N_HEADS), 0.5),
        'q_norm_g': 1.0 + nrm(ks[13], (L, N_HEADS, HEAD_DIM), 0.02),
        'k_norm_g': 1.0 + nrm(ks[14], (L, N_HEADS, HEAD_DIM), 0.02),
        'conv_w': nrm(ks[15], (L, CONV_WIDTH, D_CONV), CONV_WIDTH ** -0.5),
        'conv_b': nrm(ks[16], (L, D_CONV), 0.02),
        'conv_ln_g': 1.0 + nrm(ks[17], (L, D_CONV), 0.02),
        'conv_ln_b': nrm(ks[18], (L, D_CONV), 0.02),
        'w_out': nrm(ks[19], (L, D_MODEL, D_MODEL), D_MODEL ** -0.5),
        'norm2_g': 1.0 + nrm(ks[20], (L, D_MODEL), 0.02),
        'w_gate': nrm(ks[21], (L, D_MODEL, D_FF), D_MODEL ** -0.5),
        'w_up': nrm(ks[22], (L, D_MODEL, D_FF), D_MODEL ** -0.5),
        'w_down': nrm(ks[23], (L, D_FF, D_MODEL), D_FF ** -0.5),
    }


def reference(x_prompt, x_sample, cache_k, cache_v, cache_logf, state_conv, c_prompt, c_sample,
              w_ada, b_ada, norm1_g, w_in, b_f, q_norm_g, k_norm_g, conv_w, conv_b, conv_ln_g,
              conv_ln_b, w_out, norm2_g, w_gate, w_up, w_down):
    yp, ys = x_prompt, x_sample
    st_p, st_s = [], []
    for l in range(DEPTH):
        w = (w_ada[l], b_ada[l], norm1_g[l], w_in[l], b_f[l], q_norm_g[l], k_norm_g[l], conv_w[l],
             conv_b[l], conv_ln_g[l], conv_ln_b[l], w_out[l], norm2_g[l], w_gate[l], w_up[l], w_down[l])
        yp, sp = _layer(yp, c_prompt, None, *w)
        ys, ss = _layer(ys, c_sample, (state_conv[l], cache_k[l], cache_v[l], cache_logf[l]), *w)
        st_p.append(sp)
        st_s.append(ss)
    k_prompt = jnp.stack([s[0] for s in st_p])
    v_prompt = jnp.stack([s[1] for s in st_p])
    logf_prompt = jnp.stack([s[2] for s in st_p])
    conv_prompt = jnp.stack([s[3] for s in st_p])
    k_sample = jnp.stack([s[0] for s in st_s])
    v_sample = jnp.stack([s[1] for s in st_s])
    logf_sample = jnp.stack([s[2] for s in st_s])
    conv_sample = jnp.stack([s[3] for s in st_s])
    return (yp, ys, k_prompt, v_prompt, logf_prompt, conv_prompt, k_sample, v_sample, logf_sample, conv_sample)
```

```python
import bisect
from contextlib import ExitStack
import numpy as np
import concourse.bass as bass
import concourse.mybir as mybir
from concourse.bass_utils import run_bass_kernel_spmd

F32 = mybir.dt.float32
BF16 = mybir.dt.bfloat16
AF = mybir.ActivationFunctionType
ALU = mybir.AluOpType
AX = mybir.AxisListType

D = 1024
S = 2048
NT = 16
TS_ = 64
TT_ = S + TS_
DC = 512
NH = 8
HD = 64
CW = 31
DFF = 2816
NF = DFF // 128
INC = 2568
PAST = 1024
EPS = 1e-6
ENGS = ("sync", "scalar", "gpsimd", "vector", "tensor")
import os
STAGE = float(os.environ.get('MK_STAGE', '9'))


class Buf:
    __slots__ = ("name", "w", "r")
    EPOCH = {}

    def __init__(self, name=""):
        self.name = name
        self.w = dict(Buf.EPOCH)
        self.r = {}


class Prog:
    def __init__(self):
        self.streams = {e: [] for e in ENGS}
        self.cnt = {e: 0 for e in ENGS}
        self.seq = {e: 0 for e in ENGS}
        self.miles = {e: [] for e in ENGS}
        self.waited = {e: {} for e in ENGS}
        self.dma_cnt = {}

    def _resolve(self, tok):
        if tok[0] == 'D':
            return ("d:" + tok[1], tok[2])
        _, eng, seq = tok
        ms = self.miles[eng]
        i = bisect.bisect_left(ms, (seq, -1))
        if i >= len(ms):
            raise RuntimeError(f"unresolved milestone for {tok}")
        return ("e:" + eng, ms[i][1])

    def _wait(self, eng, tok):
        if tok[0] == 'E' and tok[1] == eng and eng == "tensor":
            return
        key, val = self._resolve(tok)
        if self.waited[eng].get(key, 0) >= val:
            return
        self.waited[eng][key] = val
        self.streams[eng].append(("wait", key, val))

    @staticmethod
    def _merge(d, key, tok):
        old = d.get(key)
        if old is None or old[2] < tok[2]:
            d[key] = tok

    def op(self, eng, fn, reads=(), writes=(), inc=True, dsem=None):
        pr = [b for b in reads if b.name.startswith("ps")]
        if pr:
            writes = list(writes) + pr
            reads = [b for b in reads if not b.name.startswith("ps")]
        deps = []
        for b in reads:
            deps.extend(b.w.values())
        for b in writes:
            deps.extend(b.w.values())
            deps.extend(b.r.values())
        for tok in deps:
            self._wait(eng, tok)
        if dsem is not None:
            v = self.dma_cnt.get(dsem, 0) + 16
            self.dma_cnt[dsem] = v
            tok = ('D', dsem, v)
            key = "d:" + dsem
            self.streams[eng].append(("op", fn, key, 16))
        else:
            self.seq[eng] += 1
            tok = ('E', eng, self.seq[eng])
            key = eng
            if inc:
                self.cnt[eng] += 1
                self.miles[eng].append((self.seq[eng], self.cnt[eng]))
                self.streams[eng].append(("op", fn, "e:" + eng, 1))
            else:
                self.streams[eng].append(("op", fn, None, 0))
        for b in reads:
            self._merge(b.r, key, tok)
        for b in writes:
            b.w = {key: tok}
            b.r = {}
        return tok

    def wait_all(self, eng, bufs):
        for b in bufs:
            for tok in list(b.w.values()) + list(b.r.values()):
                self._wait(eng, tok)

    def barrier(self):
        self.marks = getattr(self, "marks", [])
        self.marks.append(dict(self.seq))
        toks = []
        for e in ENGS:
            if self.miles[e]:
                toks.append(("e:" + e, self.miles[e][-1][1]))
        for k, v in self.dma_cnt.items():
            toks.append(("d:" + k, v))
        for e in ENGS:
            for key, val in toks:
                if key == "e:" + e and e == "tensor":
                    continue
                if self.waited[e].get(key, 0) >= val:
                    continue
                self.waited[e][key] = val
                self.streams[e].append(("wait", key, val))

    def epoch(self):
        self.marks = getattr(self, "marks", [])
        self.marks.append(dict(self.seq))
        ep = {}
        for e in ENGS:
            if e == "sync":
                continue
            if self.seq[e] > 0:
                if not (self.miles[e] and self.miles[e][-1][0] == self.seq[e]):
                    print("EPOCH WARNING: last op not a milestone on", e, self.seq[e], self.miles[e][-1] if self.miles[e] else None)
                ep[e] = ('E', e, self.seq[e])
        for k, v in self.dma_cnt.items():
            ep["d:" + k] = ('D', k, v)
        Buf.EPOCH = ep

    def emit(self, nc):
        keys = set()
        for e in ENGS:
            for it in self.streams[e]:
                if it[0] == "wait":
                    keys.add(it[1])
                elif it[2] is not None:
                    keys.add(it[2])
        keys = sorted(keys)
        with ExitStack() as st:
            sems = {k: st.enter_context(nc.semaphore(k.replace(":", "_"))) for k in keys}
            block = st.enter_context(nc.Block())

            def run(ename):
                def body(eng):
                    for it in self.streams[ename]:
                        if it[0] == "wait":
                            eng.wait_ge(sems[it[1]], it[2])
                        else:
                            inst = it[1](eng)
                            if it[2] is not None:
                                inst.then_inc(sems[it[2]], it[3])
                return body

            block.sync(run("sync"))
            block.scalar(run("scalar"))
            block.gpsimd(run("gpsimd"))
            block.vector(run("vector"))
            block.tensor(run("tensor"))


IN_SPECS = dict(
    xp=(S, D), xs=(TS_, D), ck=(4, PAST, 512), cv=(4, PAST, 512), clf=(4, PAST, NH),
    sconv=(4, 30, DC), sconvT=(128, 4, 4, 30), cT=(128, 8, 5),
    w_ada=(D, 6 * D), b_adaT=(128, 48), bada_g5=(5, 2048), n1gT=(128, 8), n2gT=(128, 8),
    w_in=(D, INC), bf_bc=(128, NH), gq_bc=(128, 512), gk_bc=(128, 512),
    conv_wT=(128, 4, CW), conv_bT=(128, 4), ln_gT=(128, 4), ln_bT=(128, 4),
    w_out=(D, D), w_gate=(D, DFF), w_up=(D, DFF), w_down=(DFF, D),
    ident=(128, 128), triU=(128, 128), ones=(128, 128), triLs=(128, 128), triUb=(64, 64),
    maskd=(128, 4, 512), masks=(64, 64), sel_p=(5, 128), sel_s=(5, 128),
)
OUT_SPECS = dict(
    yp=(S, D), ys=(TS_, D), kp=(S, 512), vp=(S, 512), lfp=(S, NH), cvp=(30, DC),
    ksm=(TS_, 512), vsm=(TS_, 512), lfs=(TS_, NH), cvs=(4, 30, DC),
)


def build_program():
    Buf.EPOCH = {}
    nc = bass.Bass("TRN2", target_bir_lowering=False)
    I = {k: nc.dram_tensor(k, list(v), F32, kind="ExternalInput").ap() for k, v in IN_SPECS.items()}
    O = {k: nc.dram_tensor(k, list(v), F32, kind="ExternalOutput").ap() for k, v in OUT_SPECS.items()}
    x1s = nc.dram_tensor("x1_scratch", [TT_, D], F32, kind="Internal").ap()
    P = Prog()
    st = ExitStack()
    ARENA_B = 212736
    arena = st.enter_context(nc.sbuf_tensor("arena", [128, ARENA_B // 4], F32))
    psum = [st.enter_context(nc.psum_tensor(f"ps{i}", [128, 512], F32)) for i in range(8)]
    psb = [Buf(f"ps{i}") for i in range(8)]
    ps_rr = [0]

    def PS():
        i = ps_rr[0] % 8
        ps_rr[0] += 1
        return psum[i], psb[i]

    def PSI(i):
        return psum[i], psb[i]

    top = [0]

    def alloc(shape, dt):
        n = int(np.prod(shape)) * (2 if dt == BF16 else 4)
        n = (n + 63) // 64 * 64
        off = top[0]
        top[0] += n
        assert top[0] <= ARENA_B, f"arena overflow {top[0]}"
        ap = arena[:, off // 4:(off + n) // 4]
        if dt == BF16:
            ap = ap.bitcast(BF16)
        ne = int(np.prod(shape))
        ap = ap[:, 0:ne]
        if len(shape) == 2:
            ap = ap.rearrange("p (a b) -> p a b", a=shape[0])
        elif len(shape) == 3:
            ap = ap.rearrange("p (a b c) -> p a b c", a=shape[0], b=shape[1])
        return ap

    def DMA(eng, out, in_, reads, writes, dsem):
        return P.op(eng, lambda e: e.dma_start(out=out, in_=in_), reads=reads, writes=writes, dsem=dsem)

    def MM(out, lhsT, rhs, start, stop, reads, writes, inc=None):
        if inc is None:
            inc = stop
        return P.op("tensor", lambda e: e.matmul(out=out, lhsT=lhsT, rhs=rhs, start=start, stop=stop,
                                                 skip_group_check=True),
                    reads=reads, writes=writes, inc=inc)

    def TR(out, in_, ident_ap, reads, writes, inc=True):
        return P.op("tensor", lambda e: e.transpose(out=out, in_=in_, identity=ident_ap),
                    reads=reads, writes=writes, inc=inc)

    def ACT(out, in_, func, reads, writes, scale=None, bias=None, accum=None):
        kw = {}
        if scale is not None:
            kw["scale"] = scale
        if bias is not None:
            kw["bias"] = bias
        if accum is not None:
            kw["accum_out"] = accum
        return P.op("scalar", lambda e: e.activation(out=out, in_=in_, func=func, **kw), reads=reads, writes=writes)

    def TSC(eng, out, in0, s1, s2, op0, op1, reads, writes):
        if s2 is None:
            return P.op(eng, lambda e: e.tensor_scalar(out=out, in0=in0, scalar1=s1, scalar2=None, op0=op0),
                        reads=reads, writes=writes)
        return P.op(eng, lambda e: e.tensor_scalar(out=out, in0=in0, scalar1=s1, scalar2=s2, op0=op0, op1=op1),
                    reads=reads, writes=writes)

    def TT(eng, out, in0, in1, op, reads, writes):
        return P.op(eng, lambda e: e.tensor_tensor(out=out, in0=in0, in1=in1, op=op), reads=reads, writes=writes)

    def STT(out, in0, scalar, in1, op0, op1, reads, writes):
        return P.op("vector", lambda e: e.scalar_tensor_tensor(out=out, in0=in0, scalar=scalar, in1=in1,
                                                               op0=op0, op1=op1), reads=reads, writes=writes)

    def CP(eng, out, in_, reads, writes):
        if eng == "scalar":
            return P.op(eng, lambda e: e.copy(out=out, in_=in_), reads=reads, writes=writes)
        return P.op(eng, lambda e: e.tensor_copy(out=out, in_=in_), reads=reads, writes=writes)

    def MSET(eng, ap, val, writes):
        return P.op(eng, lambda e: e.memset(ap, val), writes=writes)

    def RED(out, in_, reads, writes):
        return P.op("vector", lambda e: e.tensor_reduce(out=out, in_=in_, axis=AX.X, op=ALU.add),
                    reads=reads, writes=writes)

    cst = Buf("cst")
    C = {}

    cl_cnt = [0]

    def cload(name, shape, src=None, eng="sync", dt=F32):
        t = alloc(shape, dt)
        C[name] = t
        if eng == "sync":
            cl_cnt[0] += 1
            q_ = "sync" if cl_cnt[0] % 2 == 0 else "scalar"
            DMA(q_, t, I[name] if src is None else src, [], [Buf()], "cst" if q_ == "sync" else "csta")
        else:
            DMA(eng, t, I[name] if src is None else src, [], [Buf()], "cstg")
        return t

    ident = cload("ident", (128,))
    triU = cload("triU", (128,))
    ones = cload("ones", (128,))
    triLs = cload("triLs", (128,))
    cT = cload("cT", (8, 5))
    b_adaT = cload("b_adaT", (48,))
    n1gT = cload("n1gT", (8,))
    n2gT = cload("n2gT", (8,))
    bf_bc = cload("bf_bc", (NH,))
    gq_bc = cload("gq_bc", (512,))
    gk_bc = cload("gk_bc", (512,))
    conv_wT = cload("conv_wT", (4, CW))
    conv_bT = cload("conv_bT", (4,))
    ln_gT = cload("ln_gT", (4,))
    ln_bT = cload("ln_bT", (4,))
    nln_g = alloc((4,), F32)
    nln_b = alloc((4,), F32)
    identb = cload("identb", (128,), src=I["ident"], eng="gpsimd", dt=BF16)
    onesb = cload("onesb", (128,), src=I["ones"], eng="gpsimd", dt=BF16)
    triUb = alloc((64,), F32)
    DMA("sync", triUb[0:64], I["triUb"], [], [Buf()], "cst")
    masks = alloc((64,), F32)
    DMA("sync", masks[0:64], I["masks"], [], [Buf()], "cst")
    sel_p = alloc((128,), F32)
    DMA("sync", sel_p[0:5], I["sel_p"], [], [Buf()], "cst")
    sel_s = alloc((128,), F32)
    DMA("sync", sel_s[0:5], I["sel_s"], [], [Buf()], "cst")
    modTM = alloc((2048,), F32)
    negh = alloc((512,), F32)
    MSET("gpsimd", negh, -0.5, [cst])
    eps_t = alloc((1,), F32)
    MSET("gpsimd", eps_t, EPS, [cst])
    P.barrier()
    TSC("vector", nln_g, ln_gT, -1.0, None, ALU.mult, None, [cst], [cst])
    TSC("vector", nln_b, ln_bT, -1.0, None, ALU.mult, None, [cst], [cst])
    modFM = alloc((4, 8, 5), F32)
    A1 = alloc((8, 5), F32)
    A2 = alloc((8, 5), F32)
    siluT = alloc((8, 8), BF16)
    sgc = alloc((8, 5), F32)
    logf_all = alloc((17, NH), F32)
    qTs = alloc((4, 64), BF16)
    kTs = alloc((4, 64), BF16)
    Vnew = alloc((NH, 65), BF16)
    catC = alloc((4, TT_), BF16)
    hT = alloc((8, TT_), BF16)
    hTb = [[Buf() for _ in range(8)] for _ in range(17)]
    catb = [[Buf() for _ in range(5)] for _ in range(8)]
    PH = top[0]
    print("persistent bytes", PH)

    SUB = float(os.environ.get('MK_SUB', '9'))
    if SUB == 0:
        return finish(nc, P, st, O, None, None, None, None)
    modb = Buf("mod")
    MSET("vector", siluT, 0.0, [modb])
    ACT(sgc, cT, AF.Exp, [cst], [modb], scale=-1.0)
    TSC("vector", sgc, sgc, 1.0, None, ALU.add, None, [modb], [modb])
    ACT(sgc, sgc, AF.Ln, [modb], [modb])
    ACT(sgc, sgc, AF.Exp, [modb], [modb], scale=-1.0)
    TT("vector", siluT[:, :, 0:5], cT, sgc, ALU.mult, [modb], [modb])
    if SUB == 1:
        return finish(nc, P, st, O, None, None, None, None)
    arena_end = ARENA_B - 64
    save_top = top[0]
    MOD_LO = arena_end - (2 * 8 * 256 * 4 + 2 * 8 * 256 * 2 + 2048 * 4)
    top[0] = MOD_LO
    wa32 = [alloc((8, 256), F32) for _ in range(2)]
    wa16 = [alloc((8, 256), BF16) for _ in range(2)]
    wa16b = [Buf() for _ in range(2)]
    bada5 = alloc((2048,), F32)
    top[0] = save_top
    DMA("sync", bada5[0:5], I["bada_g5"], [], [Buf()], "bada5")
    wab = [Buf() for _ in range(2)]
    w_ada_v = I["w_ada"].rearrange("(k p) n -> p k n", p=128)
    modfb = Buf("modFM")
    modtb = Buf("modTM")
    P.barrier()

    def mod_dma(g):
        sl_ = g % 2
        DMA("gpsimd", wa32[sl_], w_ada_v[:, :, g * 256:(g + 1) * 256], [], [wab[sl_]], f"wa{sl_}")

    def mod_cast(g):
        sl_ = g % 2
        CP("vector", wa16[sl_], wa32[sl_], [wab[sl_]], [wa16b[sl_]])

    def mod_mm(g):
        sl_ = g % 2
        ps, pb = PS()
        kindg = g // 4
        if kindg in (0, 1, 3, 4):
            kind = {0: 0, 1: 1, 3: 2, 4: 3}[kindg]
            for cc in range(2):
                for kc in range(8):
                    MM(ps[:, cc * 8:cc * 8 + 8], wa16[sl_][:, kc, cc * 128:(cc + 1) * 128], siluT[:, kc, :],
                       kc == 0, kc == 7, [wa16b[sl_], modb], [pb])
            for cc in range(2):
                gc = g * 2 + cc
                TSC("vector", modFM[:, kind, (g % 4) * 2 + cc, :], ps[:, cc * 8:cc * 8 + 5],
                    b_adaT[:, gc:gc + 1], None, ALU.add, None, [pb], [modfb])
        else:
            for kc in range(8):
                MM(ps[0:5, 0:256], siluT[:, kc, 0:5], wa16[sl_][:, kc, :], kc == 0, kc == 7, [wa16b[sl_], modb], [pb])
            tcol = (g % 4) * 256 + (0 if kindg == 2 else 1024)
            TT("vector", modTM[0:5, tcol:tcol + 256], ps[0:5, 0:256], bada5[0:5, tcol:tcol + 256],
               ALU.add, [pb], [modtb])

    mod_dma(0)
    mod_dma(1)
    for g in range(8):
        mod_cast(g)
        mod_mm(g)
        mod_dma(g + 2)
    mod_cast(8)
    mod_cast(9)
    mod_dma(10)
    mod_dma(11)
    STT(A1, modFM[:, 1], 1.0, n1gT.unsqueeze(2).to_broadcast([128, 8, 5]), ALU.add, ALU.mult, [modfb], [modfb])
    B1 = modFM[:, 0]
    B2 = modFM[:, 2]
    if STAGE == 0:
        return finish(nc, P, st, O, None, None, None, None)

    w_in_v = I["w_in"].rearrange("(k p) n -> p k n", p=128)
    top[0] = PH
    slot0 = top[0]
    xt = [alloc((D,), F32) for _ in range(2)]
    xtb = [Buf() for _ in range(2)]
    xn = [alloc((D,), F32) for _ in range(2)]
    xnb = [Buf() for _ in range(2)]
    junk = alloc((D,), BF16)
    junkb = Buf()
    sgt = [alloc((512,), F32) for _ in range(2)]
    sgb = [Buf() for _ in range(2)]
    top[0] = max(top[0], slot0 + 8 * 1544 * 2 + 64)
    uT = alloc((4, 30 + S), BF16)
    uTs = alloc((4, 4, 46), BF16)
    uTb = [[Buf() for _ in range(5)] for _ in range(4)]
    diag = alloc((4, CW, 128), BF16)
    diagb = Buf()
    stat = alloc((17, 4), F32)
    statb = [Buf() for _ in range(17)]
    ovl = top[0]
    w_in_u = alloc((8, 1024), BF16)
    wub = Buf()
    for hh in range(2):
        DMA("gpsimd", w_in_u[:, :, hh * 512:(hh + 1) * 512], w_in_v[:, :, hh * 512:(hh + 1) * 512], [], [wub],
            "w_in_u")
    utm = alloc((512,), F32)
    utmb = Buf()
    scrb = [Buf() for _ in range(4)]
    sconv_sb = alloc((4, 4, 30), F32)
    scb = Buf()
    DMA("sync", sconv_sb, I["sconvT"], [], [scb], "sconv")
    p1_top = top[0]
    top[0] = ovl
    yf = alloc((4, 512), F32)
    yfb = [Buf() for _ in range(4)]
    ybf = alloc((4, 512), BF16)
    ysq = alloc((4, 512), BF16)
    ybb = [Buf() for _ in range(4)]
    mean_t = alloc((512,), F32)
    msq_t = alloc((512,), F32)
    rstd_t = alloc((512,), F32)
    stb = Buf()
    dtmp = alloc((512,), F32)
    vtmp = alloc((512,), F32)
    stmp = alloc((512,), F32)
    dtb = Buf()
    top[0] = max(top[0], p1_top)
    print("P1 top", top[0], "MOD_LO", MOD_LO)
    assert top[0] <= MOD_LO

    def build_diag(c):
        for j in range(CW):
            TSC("vector", diag[:, c, j, :], identb, conv_wT[:, c, j:j + 1], None, ALU.mult, None, [cst], [diagb])
    for c in range(4):
        MSET("vector", uT[:, c, 0:30], 0.0, [uTb[c][0]])
        CP("vector", uTs[:, c, :, 0:30], sconv_sb[:, c, :, :], [scb], [uTb[c][4]])

    def tile_rows(i):
        return (128, 128 * i) if i < 16 else (64, S)

    nb = {}

    def norm_A(i, slot):
        R, tok0 = tile_rows(i)
        xt, xtb, xn, xnb, junk, junkb, stat, statb = (nb[k] for k in
                                                      ("xt", "xtb", "xn", "xnb", "junk", "junkb", "stat", "statb"))
        ACT(junk[0:R], xt[slot][0:R], AF.Square, [xtb[slot]], [junkb, statb[i]], accum=stat[0:R, i, 0:1])
        ACT(stat[0:R, i, 1:2], stat[0:R, i, 0:1], AF.Ln, [statb[i], cst], [statb[i]], scale=1.0 / D, bias=eps_t[0:R])
        ACT(stat[0:R, i, 2:3], stat[0:R, i, 1:2], AF.Exp, [statb[i]], [statb[i]], scale=-0.5)
        nx = len(xn)
        ACT(xn[i % nx][0:R], xt[slot][0:R], AF.Identity, [xtb[slot], statb[i]], [xnb[i % nx]], scale=stat[0:R, i, 2:3])

    def norm_B(i, gA, gB, dst_hT, dst_bufs):
        R, tok0 = tile_rows(i)
        nx = len(nb["xn"])
        xn, xnb = nb["xn"][i % nx], nb["xnb"][i % nx]
        for half in range(2):
            ps, pb = PS()
            for q in range(4):
                kc = half * 4 + q
                TR(ps[:, q * 128:q * 128 + R], xn[0:R, kc * 128:(kc + 1) * 128], ident[0:R, 0:R], [xnb], [pb],
                   inc=(q == 3))
            for q in range(4):
                kc = half * 4 + q
                if i < 16:
                    groups = [(0, R, 0)]
                else:
                    groups = [(b * 16, 16, 1 + b) for b in range(4)]
                for (c0, n, row) in groups:
                    o = dst_hT[:, kc, tok0 + c0:tok0 + c0 + n]
                    src_ps = ps[:, q * 128 + c0:q * 128 + c0 + n]
                    if half == 0:
                        ACT(o, src_ps, AF.Identity, [pb, modfb], [dst_bufs[i][kc]],
                            scale=gA[:, kc, row:row + 1], bias=gB[:, kc, row:row + 1])
                    else:
                        STT(o, src_ps, gA[:, kc, row:row + 1], gB[:, kc, row:row + 1].to_broadcast([128, n]),
                            ALU.mult, ALU.add, [pb, modfb], [dst_bufs[i][kc]])

    glu_cnt = [0]

    def glu_block(pa, pab, pg, pgb, out_ap, out_buf, n, rows=128, fp32_out=None):
        glu_cnt[0] += 1
        s = glu_cnt[0] % 2
        ACT(sgt[s][0:rows, 0:n], pg, AF.Exp, [pgb], [sgb[s]], scale=-1.0)
        ACT(sgt[s][0:rows, 0:n], sgt[s][0:rows, 0:n], AF.Ln, [sgb[s], cst], [sgb[s]], bias=ones[0:rows, 0:1])
        ACT(sgt[s][0:rows, 0:n], sgt[s][0:rows, 0:n], AF.Exp, [sgb[s]], [sgb[s]], scale=-1.0)
        TT("vector", out_ap, pa, sgt[s][0:rows, 0:n] if out_ap.ndim == 2 else
           sgt[s][0:rows, 0:n].rearrange("p (b t) -> p b t", b=4), ALU.mult, [pab, sgb[s]], [out_buf])

    nb.update(xt=xt, xtb=xtb, xn=xn, xnb=xnb, junk=junk, junkb=junkb, stat=stat, statb=statb)
    xp_v = I["xp"]
    if SUB == 10:
        return finish(nc, P, st, O, None, None, None, None)
    def p1_post(i):
        R, tok0 = tile_rows(i)
        if i % 2 == 1 and i < 16:
            g0_ = 8 + (i - 1)
            for g_ in (g0_, g0_ + 1):
                mod_mm(g_)
            for g_ in (g0_ + 2, g0_ + 3):
                if g_ < 24:
                    mod_cast(g_)
            for g_ in (g0_ + 4, g0_ + 5):
                if g_ < 24:
                    mod_dma(g_)
            if g0_ + 1 == 19:
                STT(A2, modFM[:, 3], 1.0, n2gT.unsqueeze(2).to_broadcast([128, 8, 5]), ALU.add, ALU.mult,
                    [modfb], [modfb])
        if SUB == 12:
            return
        if i % 4 == 3 or i == 16:
            j = i // 4
            tiles = list(range(4 * j, 4 * j + 4)) if i < 16 else [16]
            n = 512 if i < 16 else 64
            t0 = 512 * j if i < 16 else S
            for c in range(4):
                pa, pab = PS()
                pg, pgb = PS()
                for (pp, ppb, coff) in ((pa, pab, 0), (pg, pgb, 512)):
                    for kc in range(8):
                        MM(pp[:, 0:n], w_in_u[:, kc, coff + c * 128:coff + (c + 1) * 128], hT[:, kc, t0:t0 + n],
                           kc == 0, kc == 7, [wub] + [hTb[t][kc] for t in tiles], [ppb])
                if i < 16:
                    glu_block(pa[:, 0:n], pab, pg[:, 0:n], pgb, uT[:, c, 30 + t0:30 + t0 + n], uTb[c][j], n)
                else:
                    glu_block(pa[:, 0:n].rearrange("p (b t) -> p b t", b=4), pab, pg[:, 0:n], pgb,
                              uTs[:, c, :, 30:46], uTb[c][4], n)
        if SUB == 13:
            return
        if i == 15 or i == 16:
            m0, M = (S - 32, 32) if i == 15 else (S, 64)
            pa, pab = PS()
            pg, pgb = PS()
            for (pp, ppb, coff) in ((pa, pab, 0), (pg, pgb, 512)):
                for kc in range(8):
                    MM(pp[0:M, :], hT[:, kc, m0:m0 + M], w_in_u[:, kc, coff:coff + 512], kc == 0, kc == 7,
                       [wub, hTb[i][kc]], [ppb])
            glu_block(pa[0:M, :], pab, pg[0:M, :], pgb, utm[0:M, 0:512], utmb, 512, rows=M)
            if i == 15:
                DMA("sync", O["cvp"], utm[2:32, 0:512], [utmb], [Buf()], "cvoutp")
            else:
                for b in range(4):
                    DMA("sync", O["cvs"][b, 14:30, :], utm[16 * b:16 * b + 16, 0:512], [utmb], [Buf()], f"cvouta{b}")
                    DMA("sync", utm[64 + 14 * b:64 + 14 * b + 14, 0:512], I["sconv"][b, 16:30, :], [], [scrb[b]], f"scr_in{b}")
                    DMA("sync", O["cvs"][b, 0:14, :], utm[64 + 14 * b:64 + 14 * b + 14, 0:512], [scrb[b]], [Buf()], f"cvoutb{b}")


    for i in range(17):
        R, tok0 = tile_rows(i)
        slot = i % 2
        src = xp_v[tok0:tok0 + R, :] if i < 16 else I["xs"]
        DMA("sync", xt[slot][0:R], src, [], [xtb[slot]], f"xt{slot}")
        norm_A(i, slot)
        if i > 0:
            norm_B(i - 1, A1, B1, hT, hTb)
        if i > 1:
            p1_post(i - 2)
        if i in (9, 11, 13, 15):
            build_diag((i - 9) // 2)
    norm_B(16, A1, B1, hT, hTb)
    p1_post(15)
    p1_post(16)
    if STAGE == 1:
        return finish(nc, P, st, O, None, None, None, None)
    P.epoch()
    top_save = top[0]
    top[0] = slot0
    w_qkv = alloc((8, 1544), BF16)
    wqb = Buf()
    for hh in range(3):
        DMA("gpsimd", w_qkv[:, :, hh * 512:(hh + 1) * 512], w_in_v[:, :, 1024 + hh * 512:1024 + (hh + 1) * 512], [],
            [wqb], "w_qkv")
    DMA("gpsimd", w_qkv[:, :, 1536:1544], w_in_v[:, :, 2560:2568], [], [wqb], "w_qkv")
    top[0] = top_save

    yfb = [Buf() for _ in range(4)]
    ybb = [Buf() for _ in range(4)]
    stb = Buf()
    dtb = Buf()
    for j in (4, 0, 1, 2, 3):
        n = 512 if j < 4 else 64
        t0 = 512 * j if j < 4 else S
        for c in range(4):
            py, pyb = PS()
            for tap in range(CW):
                if j < 4:
                    rhs = uT[:, c, t0 + tap:t0 + tap + n]
                    o = py[:, 0:n]
                    rb = [uTb[c][jj] for jj in range(max(0, j - 1), j + 1)]
                else:
                    rhs = uTs[:, c, :, tap:tap + 16]
                    o = py[:, 0:n].rearrange("p (b t) -> p b t", b=4)
                    rb = [uTb[c][4]]
                MM(o, diag[:, c, tap, :], rhs, tap == 0, tap == CW - 1, [diagb] + rb, [pyb])
            ACT(yf[:, c, 0:n], py[:, 0:n], AF.Identity, [pyb, cst], [yfb[c]], bias=conv_bT[:, c:c + 1])
            ACT(ybf[:, c, 0:n], py[:, 0:n], AF.Identity, [pyb, cst], [ybb[c]], bias=conv_bT[:, c:c + 1])
            ACT(ysq[:, c, 0:n], py[:, 0:n], AF.Square, [pyb, cst], [ybb[c]], bias=conv_bT[:, c:c + 1])
        p1, p1b = PS()
        p2, p2b = PS()
        for c in range(4):
            MM(p1[:, 0:n], onesb, ybf[:, c, 0:n], c == 0, c == 3, [ybb[c], cst], [p1b])
        for c in range(4):
            MM(p2[:, 0:n], onesb, ysq[:, c, 0:n], c == 0, c == 3, [ybb[c], cst], [p2b])
        TSC("vector", mean_t[:, 0:n], p1[:, 0:n], 1.0 / DC, None, ALU.mult, None, [p1b], [stb])
        TT("vector", msq_t[:, 0:n], mean_t[:, 0:n], mean_t[:, 0:n], ALU.mult, [stb], [stb])
        STT(msq_t[:, 0:n], p2[:, 0:n], 1.0 / DC, msq_t[:, 0:n], ALU.mult, ALU.subtract, [p2b, stb], [stb])
        ACT(rstd_t[:, 0:n], msq_t[:, 0:n], AF.Ln, [stb, cst], [stb], bias=eps_t[:, 0:1])
        ACT(rstd_t[:, 0:n], rstd_t[:, 0:n], AF.Exp, [stb], [stb], scale=-0.5)
        for c in range(4):
            TT("vector", dtmp[:, 0:n], yf[:, c, 0:n], mean_t[:, 0:n], ALU.subtract, [yfb[c], stb], [dtb])
            TT("vector", dtmp[:, 0:n], dtmp[:, 0:n], rstd_t[:, 0:n], ALU.mult, [dtb, stb], [dtb])
            ACT(stmp[:, 0:n], dtmp[:, 0:n], AF.Exp, [dtb, cst], [dtb], scale=nln_g[:, c:c + 1],
                bias=nln_b[:, c:c + 1])
            ACT(stmp[:, 0:n], stmp[:, 0:n], AF.Ln, [dtb, cst], [dtb], bias=ones[:, 0:1])
            ACT(stmp[:, 0:n], stmp[:, 0:n], AF.Exp, [dtb], [dtb], scale=-1.0)
            ACT(vtmp[:, 0:n], dtmp[:, 0:n], AF.Identity, [dtb, cst], [dtb], scale=ln_gT[:, c:c + 1],
                bias=ln_bT[:, c:c + 1])
            TT("vector", catC[:, c, t0:t0 + n], vtmp[:, 0:n], stmp[:, 0:n], ALU.mult, [dtb], [catb[c][j]])

    if STAGE == 1.5:
        return finish(nc, P, st, O, None, None, None, None)
    P.epoch()
    top[0] = slot0 + 8 * 1544 * 2 + 64
    Vaug = alloc((NT, NH, 65), BF16)
    vab = [Buf() for _ in range(NT)]
    qT = alloc((NH, S), BF16)
    kT = alloc((NH, S), BF16)
    qTb = [[Buf() for _ in range(NT)] for _ in range(NH)]
    kTb = [[Buf() for _ in range(NT)] for _ in range(NH)]
    P2S = top[0]
    Qs2 = [alloc((NH, 70), F32) for _ in range(2)]
    Ks2 = [alloc((NH, 70), F32) for _ in range(2)]
    qsb2 = [Buf(), Buf()]
    ksb2 = [Buf(), Buf()]
    kout = alloc((512,), F32)
    vout = alloc((512,), F32)
    koutb, voutb = Buf(), Buf()
    sqt = alloc((512,), F32)
    tq = alloc((512,), F32)
    sqb = Buf()
    nstat = alloc((3, NH), F32)
    nsb = Buf()
    zt = alloc((17, NH), F32)
    et = alloc((17, NH), F32)
    spre = alloc((16, NH), F32)
    cum = alloc((16, NH), F32)
    hb = alloc((16 * NH,), BF16)
    h32 = alloc((16 * NH,), F32)
    r1 = alloc((16 * NH,), F32)
    augQ = alloc((16, NH, 6), F32)
    augK = alloc((16, NH, 6), F32)
    lb = Buf("logf")
    augb = Buf("aug")
    Qss = alloc((512,), F32)
    Kss = alloc((512,), F32)
    sab = Buf("sample_misc")
    print("P2 top", top[0])

    pf, pfb = PS()
    for i in range(17):
        R, tok0 = tile_rows(i)
        for kc in range(8):
            MM(pf[0:R, i * 8:(i + 1) * 8], hT[:, kc, tok0:tok0 + R], w_qkv[:, kc, 1536:1544], kc == 0, kc == 7,
               [wqb, hTb[i][kc]], [pfb])
    for (r0, r1_, i0, i1) in ((0, 128, 0, 16), (0, 64, 16, 17)):
        ni = i1 - i0
        TT("vector", zt[r0:r1_, i0:i1, :], pf[r0:r1_, i0 * 8:i1 * 8].rearrange("p (i h) -> p i h", h=NH),
           bf_bc[r0:r1_].unsqueeze(1).to_broadcast([r1_ - r0, ni, NH]), ALU.add, [pfb, cst], [lb])
        ACT(et[r0:r1_, i0:i1, :], zt[r0:r1_, i0:i1, :], AF.Exp, [lb], [lb], scale=-1.0)
        TSC("vector", et[r0:r1_, i0:i1, :], et[r0:r1_, i0:i1, :], 1.0, None, ALU.add, None, [lb], [lb])
        ACT(zt[r0:r1_, i0:i1, :], et[r0:r1_, i0:i1, :], AF.Ln, [lb], [lb])
        TSC("vector", logf_all[r0:r1_, i0:i1, :], zt[r0:r1_, i0:i1, :], -1.0, None, ALU.mult, None, [lb], [lb])
    DMA("sync", O["lfp"].rearrange("(i p) h -> p i h", p=128), logf_all[:, 0:16, :], [lb], [Buf()], "lfout")
    DMA("sync", O["lfs"], logf_all[0:64, 16, :], [lb], [Buf()], "lfout")
    MSET("vector", spre[:, 0, :], 0.0, [lb])
    for i in range(1, 16):
        TT("vector", spre[:, i, :], spre[:, i - 1, :], logf_all[:, i - 1, :], ALU.add, [lb], [lb])
    pc, pcb = PS()
    for i in range(16):
        MM(pc[:, i * 8:(i + 1) * 8], triU, logf_all[:, i, :], True, i == 0, [lb, cst], [pcb], inc=(i == 0))
        if i > 0:
            MM(pc[:, i * 8:(i + 1) * 8], ones, spre[:, i, :], False, True, [lb, cst], [pcb])
    cumf = cum.rearrange("p i h -> p (i h)")
    CP("vector", cumf, pc[:, 0:128], [pcb], [augb])
    MSET("gpsimd", augQ, 1.0, [augb])
    MSET("gpsimd", augK, 1.0, [augb])
    aQ = augQ.rearrange("p i h r -> p (i h) r")
    aK = augK.rearrange("p i h r -> p (i h) r")
    cur = cumf
    for lvl in range(3):
        CP("vector", hb, cur, [augb], [augb])
        CP("vector", h32, hb, [augb], [augb])
        CP("vector", aQ[:, :, lvl], h32, [augb], [augb])
        TSC("vector", aK[:, :, 3 + lvl], h32, -1.0, None, ALU.mult, None, [augb], [augb])
        if lvl < 2:
            TT("vector", r1, cur, h32, ALU.subtract, [augb], [augb])
            cur = r1

    for i in range(NT):
        MSET("gpsimd", Vaug[:, i, :, 64:65], 1.0, [vab[i]])
    MSET("gpsimd", Vnew[:, :, 64:65], 1.0, [sab])

    def qk_norm(psrc, pb, R, g_bc, pre_scale, out_ap, out_buf):
        ACT(sqt[0:R], psrc[0:R], AF.Square, [pb], [sqb])
        RED(nstat[0:R, 0, :], sqt[0:R].rearrange("p (h d) -> p h d", h=NH), [sqb], [nsb])
        ACT(nstat[0:R, 1, :], nstat[0:R, 0, :], AF.Ln, [nsb, cst], [nsb], scale=1.0 / HD, bias=eps_t[0:R])
        ACT(nstat[0:R, 2, :], nstat[0:R, 1, :], AF.Exp, [nsb], [nsb], scale=-0.5)
        TT("vector", tq[0:R].rearrange("p (h d) -> p h d", h=NH), psrc[0:R].rearrange("p (h d) -> p h d", h=NH),
           nstat[0:R, 2, :].unsqueeze(2).to_broadcast([R, NH, HD]), ALU.mult, [pb, nsb], [sqb])
        gv = g_bc[0:R] if out_ap.ndim == 2 else g_bc[0:R].rearrange("p (h d) -> p h d", h=NH)
        tv = tq[0:R] if out_ap.ndim == 2 else tq[0:R].rearrange("p (h d) -> p h d", h=NH)
        STT(out_ap, tv, pre_scale, gv, ALU.mult, ALU.mult, [sqb, cst], [out_buf])

    p2ps = {}
    trc = [0]

    def p2_mm(i):
        R, tok0 = tile_rows(i)
        pq, pqb = PS()
        pk, pkb = PS()
        pv, pvb = PS()
        p2ps[i] = (pq, pqb, pk, pkb, pv, pvb)
        for (pp, ppb, coff) in ((pq, pqb, 0), (pk, pkb, 512), (pv, pvb, 1024)):
            for kc in range(8):
                MM(pp[0:R, :], hT[:, kc, tok0:tok0 + R], w_qkv[:, kc, coff:coff + 512], kc == 0, kc == 7,
                   [wqb, hTb[i][kc]], [ppb])

    def p2_chain(i):
        R, tok0 = tile_rows(i)
        pq, pqb, pk, pkb, pv, pvb = p2ps[i]
        Qs, Ks, qsb, ksb = Qs2[i % 2], Ks2[i % 2], qsb2[i % 2], ksb2[i % 2]
        if i < 16:
            qk_norm(pq, pqb, R, gq_bc, HD ** -0.5, Qs[:, :, 0:64], qsb)
            CP("gpsimd", Qs[:, :, 64:70], augQ[:, i], [augb], [qsb])
            qk_norm(pk, pkb, R, gk_bc, 1.0, kout[0:R], koutb)
            DMA("sync", O["kp"][tok0:tok0 + R, :], kout[0:R], [koutb], [Buf()], "kout")
            CP("scalar", Ks[:, :, 0:64], kout.rearrange("p (h d) -> p h d", h=NH), [koutb], [ksb])
            CP("gpsimd", Ks[:, :, 64:70], augK[:, i], [augb], [ksb])
            CP("scalar", vout[0:R], pv[0:R], [pvb], [voutb])
            DMA("sync", O["vp"][tok0:tok0 + R, :], vout[0:R], [voutb], [Buf()], "vout")
            CP("vector", Vaug[:, i, :, 0:64], vout.rearrange("p (h d) -> p h d", h=NH), [voutb], [vab[i]])
        else:
            qk_norm(pq, pqb, R, gq_bc, HD ** -0.5, Qss[0:R], sab)
            qk_norm(pk, pkb, R, gk_bc, 1.0, Kss[0:R], sab)
            DMA("sync", O["ksm"], Kss[0:R], [sab], [Buf()], "ksout")
            CP("scalar", vout[0:R], pv[0:R], [pvb], [voutb])
            DMA("sync", O["vsm"], vout[0:R], [voutb], [Buf()], "vout")
            CP("vector", Vnew[0:R, :, 0:64], vout[0:R].rearrange("p (h d) -> p h d", h=NH), [voutb], [sab])

    def p2_tr(i):
        R, tok0 = tile_rows(i)
        Qs, Ks, qsb, ksb = Qs2[i % 2], Ks2[i % 2], qsb2[i % 2], ksb2[i % 2]
        if i < 16:
            for (src_s, srcb, dstT, dstb) in ((Qs, qsb, qT, qTb), (Ks, ksb, kT, kTb)):
                for hg in range(2):
                    ps, pb = PS()
                    for hq in range(4):
                        h = hg * 4 + hq
                        TR(ps[0:70, hq * 128:(hq + 1) * 128], src_s[:, h, :], ident, [srcb, cst], [pb], inc=(hq == 3))
                    eng = "scalar" if hg == 0 else "vector"
                    CP(eng, dstT[0:70, hg * 4:hg * 4 + 4, tok0:tok0 + 128],
                       ps[0:70, :].rearrange("p (h t) -> p h t", h=4), [pb], [dstb[hg * 4 + q_][i] for q_ in range(4)])
        else:
            for (src_s, dstT) in ((Qss, qTs), (Kss, kTs)):
                ps, pb = PS()
                for pr in range(4):
                    TR(ps[:, pr * 64:(pr + 1) * 64], src_s[0:R, pr * 128:(pr + 1) * 128], ident[0:R, 0:R], [sab, cst],
                       [pb], inc=(pr == 3))
                CP("vector", dstT, ps[:, 0:256].rearrange("p (a t) -> p a t", a=4), [pb], [sab])

    for i in range(17):
        p2_mm(i)
        p2_chain(i)
        if i > 0:
            p2_tr(i - 1)
    p2_tr(16)

    if STAGE < 3:
        return finish(nc, P, st, O, None, None, None, None)

    P.epoch()
    top[0] = slot0
    catA = alloc((4, TT_), BF16)
    catAb = [[Buf() for _ in range(5)] for _ in range(4)]
    maskd = alloc((4, 512), BF16)
    mkb = Buf()
    DMA("gpsimd", maskd, I["maskd"], [], [mkb], "maskd")
    assert top[0] <= slot0 + 8 * 1544 * 2 + 64
    top[0] = P2S
    w_out_v = I["w_out"].rearrange("(k p) n -> p k n", p=128)
    w_o = alloc((8, 1024), BF16)
    wob = Buf()
    for hh in range(2):
        DMA("gpsimd", w_o[:, :, hh * 512:(hh + 1) * 512], w_out_v[:, :, hh * 512:(hh + 1) * 512], [], [wob], "w_out")
    PT = [alloc((512,), BF16) for _ in range(4)]
    PTb = [Buf() for _ in range(4)]
    Rbc = [alloc((512,), F32) for _ in range(2)]
    Rbb = [Buf() for _ in range(2)]
    rsrow = alloc((512,), F32)
    rsb = Buf()
    lnr = alloc((512,), F32)
    lnb = Buf()
    clf_sb = alloc((4, 8, NH), F32)
    clfb = Buf()
    for b in range(4):
        DMA("sync", clf_sb[:, b], I["clf"][b].rearrange("(c p) h -> p c h", p=128), [], [clfb], "clf")
    P3S = top[0]
    print("P3 top", top[0])

    def normalize(po, pob, ncols, out_fn, slot):
        CP("vector", rsrow[64:65, 0:ncols], po[64:65, 0:ncols], [pob], [rsb])
        pbc, pbcb = PSI(6 + slot)
        MM(pbc[0:64, 0:ncols], ones[64:65, 0:64], rsrow[64:65, 0:ncols], True, True, [rsb, cst], [pbcb])
        P.op("vector", lambda e: e.reciprocal(out=Rbc[slot][0:64, 0:ncols], in_=pbc[0:64, 0:ncols]),
             reads=[pbcb], writes=[Rbb[slot]])
        out_fn(Rbc[slot], Rbb[slot])

    its = []
    hj = 0
    for h in range(NH):
        for j in range(4):
            nkc = 4 * j + 4
            for c in range(nkc):
                its.append((h, j, c, nkc, hj))
            hj += 1

    def emit_qk(k):
        h, j, c, nkc, hj_ = its[k]
        r = c - 4 * j
        q0 = 128 * r if r > 0 else 0
        pss, pssb = PSI(k % 4)
        qb = [qTb[h][t] for t in range(4 * j + q0 // 128, 4 * j + 4)]
        MM(pss[:, q0:512], kT[0:70, h, c * 128:(c + 1) * 128], qT[0:70, h, 512 * j + q0:512 * j + 512],
           True, r < 0, [kTb[h][c]] + qb, [pssb])
        if r >= 0:
            MM(pss[:, q0:q0 + 128], identb, maskd[:, r, q0:q0 + 128], False, True, [mkb, cst], [pssb])
        ACT(PT[k % 4][:, q0:512], pss[:, q0:512], AF.Exp, [pssb], [PTb[k % 4]])

    def emit_pv(k):
        h, j, c, nkc, hj_ = its[k]
        r = c - 4 * j
        q0 = 128 * r if r > 0 else 0
        po, pob = PSI(4 + hj_ % 2)
        MM(po[0:65, q0:512], Vaug[:, c, h, :], PT[k % 4][:, q0:512], c == 0, c == nkc - 1, [vab[c], PTb[k % 4]],
           [pob])
        if c == nkc - 1:
            def outp(R_, Rb_, h=h, j=j, po=po, pob=pob):
                TT("vector", catA[64 * (h % 2):64 * (h % 2) + 64, h // 2, 512 * j:512 * j + 512], po[0:64, :],
                   R_[0:64, :], ALU.mult, [pob, Rb_], [catAb[h // 2][j]])
            pending.append((k, lambda po=po, pob=pob, outp=outp, hj_=hj_: normalize(po, pob, 512, outp, hj_ % 2)))
        while pending and pending[0][0] <= k:
            pending.pop(0)[1]()

    LOOKAHEAD = 3
    pending = []
    for k in range(len(its) + LOOKAHEAD):
        if k < len(its):
            emit_qk(k)
        if k - LOOKAHEAD >= 0:
            emit_pv(k - LOOKAHEAD)
    while pending:
        pending.pop(0)[1]()

    if STAGE < 3.5:
        return finish(nc, P, st, O, None, None, None, None)

    P.epoch()
    top[0] = slot0 + 8 * 1544 * 2 + 64
    kraw = [alloc((512,), F32) for _ in range(4)]
    krb = [Buf() for _ in range(4)]
    vraw = [alloc((512,), F32) for _ in range(4)]
    vrb = [Buf() for _ in range(4)]
    kTc = alloc((4, PAST), BF16)
    kTcb = [Buf() for _ in range(8)]
    Vc2 = [alloc((8, NH, 65), BF16) for _ in range(2)]
    Vcb2 = [[Buf() for _ in range(8)] for _ in range(2)]
    spost = alloc((4, 8, NH), F32)
    Ecache = alloc((4, 8, NH), F32)
    Enew = alloc((NH,), F32)
    en32 = alloc((NH, 64), F32)
    PTn = alloc((NH, 64), BF16)
    PTs = alloc((8, NH, 16), BF16)
    PTsb = [Buf() for _ in range(8)]
    sb2 = Buf("sample2")
    assert top[0] <= P2S
    MSET("vector", spost[:, :, 7, :], 0.0, [sb2])
    for c in range(6, -1, -1):
        TT("vector", spost[:, :, c, :], spost[:, :, c + 1, :], clf_sb[:, :, c + 1, :], ALU.add, [sb2, clfb], [sb2])
    pt_, ptb = PSI(0)
    MM(pt_[:, 0:256], triLs, clf_sb.rearrange("p b c h -> p (b c h)"), True, False, [sb2, clfb, cst], [ptb], inc=False)
    MM(pt_[:, 0:256], ones, spost.rearrange("p b c h -> p (b c h)"), False, True, [sb2, cst], [ptb])
    ACT(Ecache.rearrange("p b c h -> p (b c h)"), pt_[:, 0:256], AF.Exp, [ptb], [sb2])
    pcq, pcqb = PSI(1)
    MM(pcq[0:64, 0:8], triUb[0:64, 0:64], logf_all[0:64, 16, :], True, True, [lb, cst], [pcqb])
    ACT(Enew[0:64], pcq[0:64, 0:8], AF.Exp, [pcqb], [sb2], scale=-1.0)
    TT("vector", Vnew[0:64], Vnew[0:64], Enew[0:64].unsqueeze(2).to_broadcast([64, NH, 65]), ALU.mult,
       [sab, sb2], [sab])
    if SUB == 20:
        return finish(nc, P, st, O, None, None, None, None)
    psn2 = [PSI(2), PSI(1)]
    en4 = en32.rearrange("p (a two) q -> p a two q", two=2)
    for par in range(2):
        psn, psnb = psn2[par]
        pp = 64 * par
        for a in range(4):
            MM(psn[0:64, a * 64:(a + 1) * 64], kTs[pp:pp + 64, a, :], qTs[pp:pp + 64, a, :], True, True,
               [sab], [psnb], inc=(a == 3))
        ACT(en4[0:64, :, par, :], psn[0:64, 0:256].rearrange("p (a q) -> p a q", a=4), AF.Exp, [psnb], [sb2])
    TT("vector", PTn[0:64], en32[0:64], masks[0:64].unsqueeze(1).to_broadcast([64, NH, 64]), ALU.mult, [sb2, cst],
       [sb2])
    if SUB == 21:
        return finish(nc, P, st, O, None, None, None, None)
    pos, posb = PSI(3)
    ck_v = I["ck"]
    cv_v = I["cv"]
    def p3b_T(k):
        b, c = divmod(k, 8)
        s_ = k % 4
        Vc, Vcb = Vc2[b % 2], Vcb2[b % 2]
        DMA("sync", kraw[s_], ck_v[b, c * 128:(c + 1) * 128, :], [], [krb[s_]], f"kraw{s_}")
        DMA("sync", vraw[s_], cv_v[b, c * 128:(c + 1) * 128, :], [], [vrb[s_]], f"vraw{s_}")
        ptr, ptrb = PSI(4 + k % 2)
        for pr in range(4):
            TR(ptr[:, pr * 128:(pr + 1) * 128], kraw[s_][:, pr * 128:(pr + 1) * 128], ident, [krb[s_], cst],
               [ptrb], inc=(pr == 3))
        CP("vector", kTc[:, :, c * 128:(c + 1) * 128], ptr[:, :].rearrange("p (a t) -> p a t", a=4), [ptrb],
           [kTcb[c]])
        TT("vector", Vc[:, c, :, 0:64], vraw[s_].rearrange("p (h d) -> p h d", h=NH),
           Ecache[:, b, c, :].unsqueeze(2).to_broadcast([128, NH, HD]), ALU.mult, [vrb[s_], sb2], [Vcb[c]])
        CP("vector", Vc[:, c, :, 64:65], Ecache[:, b, c, :].unsqueeze(2), [sb2], [Vcb[c]])

    def p3b_S(k):
        b, c = divmod(k, 8)
        PT4 = PTs[:, c].rearrange("p (a two) q -> p a two q", two=2)
        for par in range(2):
            pss, pssb = PSI(6 + par)
            pp = 64 * par
            for a in range(4):
                MM(pss[:, a * 16:(a + 1) * 16], kTc[pp:pp + 64, a, c * 128:(c + 1) * 128],
                   qTs[pp:pp + 64, a, b * 16:(b + 1) * 16], True, True, [kTcb[c], sab], [pssb], inc=(a == 3))
            ACT(PT4[:, :, par, :], pss[:, 0:64].rearrange("p (a q) -> p a q", a=4), AF.Exp, [pssb], [PTsb[c]])

    def p3b_PV(b):
        nonlocal_first = first_flag
        Vc, Vcb = Vc2[b % 2], Vcb2[b % 2]
        for h in range(NH):
            o = pos[0:65, (b * 8 + h) * 16:(b * 8 + h) * 16 + 16]
            for c in range(8):
                MM(o, Vc[:, c, h, :], PTs[:, c, h, :], nonlocal_first[0], False, [Vcb[c], PTsb[c]], [posb], inc=False)
                nonlocal_first[0] = False
            MM(o, Vnew[0:64, h, :], PTn[0:64, h, b * 16:(b + 1) * 16], False, True, [sab, sb2], [posb])

    first_flag = [True]
    p3b_T(0)
    for k in range(32):
        if k + 1 < 32:
            p3b_T(k + 1)
        p3b_S(k)
        if k % 8 == 7:
            p3b_PV(k // 8)

    def outs(R_, Rb_):
        for h in range(NH):
            pp = 64 * (h % 2)
            TT("vector", catA[pp:pp + 64, h // 2, S:S + 64].rearrange("p (b q) -> p b q", b=4),
               pos[0:64, :].rearrange("p (b h q) -> p b h q", b=4, h=NH)[:, :, h, :],
               R_[0:64, :].rearrange("p (b h q) -> p b h q", b=4, h=NH)[:, :, h, :], ALU.mult, [posb, Rb_],
               [catAb[h // 2][4]])
    if SUB in (22, 23, 24):
        return finish(nc, P, st, O, None, None, None, None)
    normalize(pos, posb, 512, outs, 0)

    if STAGE < 4:
        return finish(nc, P, st, O, None, None, None, None)

    P.epoch()
    top[0] = slot0 + 8 * 1544 * 2 + 64
    G1 = [alloc((D,), F32) for _ in range(2)]
    G1b = Buf()
    xin = [alloc((D,), F32) for _ in range(3)]
    xinb = [Buf() for _ in range(3)]
    x1t = [alloc((D,), F32) for _ in range(3)]
    x1b = [Buf() for _ in range(3)]
    xn4 = [alloc((D,), F32) for _ in range(3)]
    junk4 = alloc((D,), BF16)
    stat4 = alloc((17, 4), F32)
    assert top[0] <= P2S
    nb.update(xt=x1t, xtb=x1b, xn=xn4, xnb=[Buf(), Buf(), Buf()], junk=junk4, junkb=Buf(), stat=stat4,
              statb=[Buf() for _ in range(17)])

    def gate_tiles(G, Gb, col0):
        for gi, (sel, M) in enumerate(((sel_p, 128), (sel_s, 64))):
            for half in range(2):
                ps, pb = PS()
                MM(ps[0:M, :], sel[0:5, 0:M], modTM[0:5, col0 + half * 512:col0 + (half + 1) * 512], True, True,
                   [modtb, cst], [pb])
                CP("vector", G[gi][0:M, half * 512:(half + 1) * 512], ps[0:M, :], [pb], [Gb])

    gate_tiles(G1, G1b, 0)
    h2b = [[Buf() for _ in range(8)] for _ in range(17)]
    x1s_b = [Buf() for _ in range(17)]
    for i in range(17):
        R, tok0 = tile_rows(i)
        slot = i % 3
        ci = i // 4 if i < 16 else 4
        src = xp_v[tok0:tok0 + R, :] if i < 16 else I["xs"]
        DMA("sync", xin[slot][0:R], src, [], [xinb[slot]], f"xin{slot}")
        Gt = G1[0] if i < 16 else G1[1]
        for half in range(2):
            ps, pb = PS()
            for kc in range(8):
                lhs = catC[:, kc, tok0:tok0 + R] if kc < 4 else catA[:, kc - 4, tok0:tok0 + R]
                rb = catb[kc][ci] if kc < 4 else catAb[kc - 4][ci]
                MM(ps[0:R, :], lhs, w_o[:, kc, half * 512:(half + 1) * 512], kc == 0, kc == 7, [rb, wob], [pb])
            TT("vector", x1t[slot][0:R, half * 512:(half + 1) * 512], ps[0:R, :], Gt[0:R, half * 512:(half + 1) * 512],
               ALU.mult, [pb, G1b], [x1b[slot]])
        TT("vector", x1t[slot][0:R], x1t[slot][0:R], xin[slot][0:R], ALU.add, [xinb[slot]], [x1b[slot]])
        DMA("sync", x1s[tok0:tok0 + R, :], x1t[slot][0:R], [x1b[slot]], [x1s_b[i]], f"x1spill{slot}")
        norm_A(i, slot)
        if i > 1:
            norm_B(i - 2, A2, B2, hT, h2b)
    norm_B(15, A2, B2, hT, h2b)
    norm_B(16, A2, B2, hT, h2b)

    if STAGE < 5:
        return finish(nc, P, st, O, None, None, None, None)

    P.epoch()
    top[0] = slot0
    hid = alloc((NF, TT_), BF16)
    hidb = [[Buf() for _ in range(5)] for _ in range(NF)]
    wgu_off = top[0]
    wg = [alloc((8, 256), BF16) for _ in range(2)]
    wu = [alloc((8, 256), BF16) for _ in range(2)]
    wgb = [Buf() for _ in range(2)]
    wub2 = [Buf() for _ in range(2)]
    sil = [alloc((512,), F32) for _ in range(2)]
    silb = [Buf() for _ in range(2)]
    wd0 = alloc((NF, 512), BF16)
    wdb = [Buf(), Buf()]
    print("P5 top", top[0])
    w_gate_v = I["w_gate"].rearrange("(k p) n -> p k n", p=128)
    w_up_v = I["w_up"].rearrange("(k p) n -> p k n", p=128)
    w_down_v = I["w_down"].rearrange("(f p) n -> p f n", p=128)
    chunks = [(512 * j, 512, [4 * j + t for t in range(4)]) for j in range(4)] + [(S, 64, [16])]
    sc = 0
    for f in range(NF):
        g2, fo = f // 2, f % 2
        s_ = g2 % 2
        if fo == 0:
            DMA("gpsimd", wg[s_], w_gate_v[:, :, g2 * 256:g2 * 256 + 256], [], [wgb[s_]], f"wg{s_}")
            DMA("gpsimd", wu[s_], w_up_v[:, :, g2 * 256:g2 * 256 + 256], [], [wub2[s_]], f"wu{s_}")
        if f == 3:
            for (f0, f1) in ((0, 8), (8, 16), (16, NF)):
                DMA("gpsimd", wd0[:, f0:f1, :], w_down_v[:, f0:f1, 0:512], [], [wdb[0]], "wd0")
        for ci, (t0, n, tiles) in enumerate(chunks):
            pg, pgb = PS()
            pu, pub = PS()
            for (pp, ppb, w_, wb_) in ((pg, pgb, wg[s_], wgb[s_]), (pu, pub, wu[s_], wub2[s_])):
                for kc in range(8):
                    MM(pp[:, 0:n], w_[:, kc, fo * 128:(fo + 1) * 128], hT[:, kc, t0:t0 + n], kc == 0, kc == 7,
                       [wb_] + [h2b[t][kc] for t in tiles], [ppb])
            k_ = sc % 2
            sc += 1
            ACT(sil[k_][:, 0:n], pg[:, 0:n], AF.Silu, [pgb], [silb[k_]])
            TT("vector", hid[:, f, t0:t0 + n], pu[:, 0:n], sil[k_][:, 0:n], ALU.mult, [pub, silb[k_]], [hidb[f][ci]])

    P.epoch()
    top[0] = wgu_off
    yt = [alloc((512,), F32) for _ in range(2)]
    ytb = [Buf() for _ in range(2)]
    x1r = [alloc((512,), F32) for _ in range(2)]
    x1rb = [Buf() for _ in range(2)]
    print("P5b top", top[0])
    wd1 = hT.rearrange("p a t -> p (a t)")[:, 0:NF * 512].rearrange("p (f n) -> p f n", f=NF)
    wdb[1] = Buf()
    cflat = catC.rearrange("p a t -> p (a t)")
    G2 = [cflat[:, 0:2 * D].bitcast(F32), cflat[:, 2 * D:4 * D].bitcast(F32)]
    G2b = Buf()
    wd = [wd0, wd1]
    for (f0, f1) in ((0, 8), (8, 16), (16, NF)):
        DMA("gpsimd", wd1[:, f0:f1, :], w_down_v[:, f0:f1, 512:1024], [], [wdb[1]], "wd1")
    gate_tiles(G2, G2b, 1024)
    cnt5 = 0
    for half in range(2):
        for i in range(17):
            R, tok0 = tile_rows(i)
            slot = cnt5 % 2
            cnt5 += 1
            ci = i // 4 if i < 16 else 4
            DMA("sync", x1r[slot][0:R], x1s[tok0:tok0 + R, half * 512:(half + 1) * 512], [x1s_b[i]], [x1rb[slot]],
                f"x1r{slot}")
            Gt = G2[0] if i < 16 else G2[1]
            ps, pb = PS()
            for f in range(NF):
                MM(ps[0:R, :], hid[:, f, tok0:tok0 + R], wd[half][:, f, :], f == 0, f == NF - 1,
                   [hidb[f][ci], wdb[half]], [pb])
            TT("vector", yt[slot][0:R], ps[0:R, :], Gt[0:R, half * 512:(half + 1) * 512], ALU.mult, [pb, G2b],
               [ytb[slot]])
            TT("vector", yt[slot][0:R], yt[slot][0:R], x1r[slot][0:R], ALU.add, [x1rb[slot]], [ytb[slot]])
            dst = O["yp"][tok0:tok0 + R, half * 512:(half + 1) * 512] if i < 16 else O["ys"][:, half * 512:(half + 1) * 512]
            DMA("sync", dst, yt[slot][0:R], [ytb[slot]], [Buf()], f"yout{slot}")
    return finish(nc, P, st, O, None, None, None, None)


def finish(nc, P, st, O, zero_out, alloc, MSET, DMA):
    P.barrier()
    print("MARKS", [(m["tensor"], m["scalar"], m["vector"], m["gpsimd"]) for m in P.marks])
    P.emit(nc)
    st.close()
    return nc


_CACHE = {}


def _consts():
    c = {}
    c["ident"] = np.eye(128, dtype=np.float32)
    r = np.arange(128)
    c["triU"] = (r[:, None] <= r[None, :]).astype(np.float32)
    c["ones"] = np.ones((128, 128), np.float32)
    c["triLs"] = (r[:, None] > r[None, :]).astype(np.float32)
    r64 = np.arange(64)
    same = (r64[:, None] // 16) == (r64[None, :] // 16)
    c["triUb"] = (same & (r64[:, None] <= r64[None, :])).astype(np.float32)
    c["masks"] = (same & (r64[:, None] <= r64[None, :])).astype(np.float32)
    md = np.zeros((128, 4, 512), np.float32)
    q = np.arange(512)
    for rr in range(4):
        md[:, rr, :] = np.where(128 * rr + r[:, None] > q[None, :], -1e30, 0.0)
    c["maskd"] = md
    sp = np.zeros((5, 128), np.float32)
    sp[0, :] = 1.0
    ss = np.zeros((5, 128), np.float32)
    for m in range(64):
        ss[1 + m // 16, m] = 1.0
    c["sel_p"] = sp
    c["sel_s"] = ss
    return c


def kernel(x_prompt, x_sample, cache_k, cache_v, cache_logf, state_conv, c_prompt, c_sample,
           w_ada, b_ada, norm1_g, w_in, b_f, q_norm_g, k_norm_g, conv_w, conv_b, conv_ln_g,
           conv_ln_b, w_out, norm2_g, w_gate, w_up, w_down):
    f = lambda a: np.ascontiguousarray(np.asarray(a, dtype=np.float32))
    x_prompt, x_sample, cache_k, cache_v, cache_logf, state_conv = map(f, (x_prompt, x_sample, cache_k, cache_v,
                                                                           cache_logf, state_conv))
    c_prompt, c_sample = f(c_prompt), f(c_sample)
    if "nc" not in _CACHE:
        _CACHE["nc"] = build_program()
    nc = _CACHE["nc"]
    cst = _consts()

    def fm(v, nch):
        return f(np.asarray(v).reshape(nch, 128).T)

    shared = dict(
        w_ada=f(w_ada[0]), b_adaT=fm(b_ada[0], 48),
        bada_g5=f(np.tile(np.concatenate([b_ada[0][2048:3072], b_ada[0][5120:6144]])[None, :], (5, 1))),
        n1gT=fm(norm1_g[0], 8), n2gT=fm(norm2_g[0], 8), w_in=f(w_in[0]),
        bf_bc=f(np.tile(np.asarray(b_f[0])[None, :], (128, 1))),
        gq_bc=f(np.tile(np.asarray(q_norm_g[0]).reshape(1, 512), (128, 1))),
        gk_bc=f(np.tile(np.asarray(k_norm_g[0]).reshape(1, 512), (128, 1))),
        conv_wT=f(np.asarray(conv_w[0]).T.reshape(4, 128, CW).transpose(1, 0, 2)),
        conv_bT=fm(conv_b[0], 4), ln_gT=fm(conv_ln_g[0], 4), ln_bT=fm(conv_ln_b[0], 4),
        w_out=f(w_out[0]), w_gate=f(w_gate[0]), w_up=f(w_up[0]), w_down=f(w_down[0]), **cst)
    in_maps = []
    for i in range(8):
        sb = slice(4 * i, 4 * i + 4)
        crow = np.concatenate([c_prompt[i:i + 1], c_sample[sb]], 0)
        m = dict(shared)
        m.update(
            xp=x_prompt[i], xs=f(x_sample[sb].reshape(64, D)),
            ck=f(cache_k[0, sb].reshape(4, PAST, 512)), cv=f(cache_v[0, sb].reshape(4, PAST, 512)),
            clf=f(cache_logf[0, sb]), sconv=f(state_conv[0, sb]),
            sconvT=f(state_conv[0, sb].transpose(2, 0, 1).reshape(4, 128, 4, 30).transpose(1, 0, 2, 3)),
            cT=f(crow.T.reshape(8, 128, 5).transpose(1, 0, 2)),
        )
        in_maps.append(m)
    res = run_bass_kernel_spmd(nc, in_maps, core_ids=list(range(8)))
    R = res.results
    g = lambda k: np.stack([np.asarray(r[k], dtype=np.float32) for r in R], 0)
    yp = g("yp")
    ys = g("ys").reshape(32, 16, D)
    kp = g("kp").reshape(1, 8, S, NH, HD)
    vp = g("vp").reshape(1, 8, S, NH, HD)
    lfp = g("lfp").reshape(1, 8, S, NH)
    cvp = g("cvp").reshape(1, 8, 30, DC)
    ksm = g("ksm").reshape(1, 32, 16, NH, HD)
    vsm = g("vsm").reshape(1, 32, 16, NH, HD)
    lfs = g("lfs").reshape(1, 32, 16, NH)
    cvs = g("cvs").reshape(1, 32, 30, DC)
    return (yp, ys, kp, vp, lfp, cvp, ksm, vsm, lfs, cvs)
```

```python
import bisect
from contextlib import ExitStack
import numpy as np
import concourse.bass as bass
import concourse.mybir as mybir
from concourse.bass_utils import run_bass_kernel_spmd

F32 = mybir.dt.float32
BF16 = mybir.dt.bfloat16
AF = mybir.ActivationFunctionType
ALU = mybir.AluOpType
AX = mybir.AxisListType

D = 1024
S = 2048
NT = 16
TS_ = 64
TT_ = S + TS_
DC = 512
NH = 8
HD = 64
CW = 31
DFF = 2816
NF = DFF // 128
INC = 2568
PAST = 1024
EPS = 1e-6
ENGS = ("sync", "scalar", "gpsimd", "vector", "tensor")
import os
STAGE = float(os.environ.get('MK_STAGE', '9'))


class Buf:
    __slots__ = ("name", "w", "r")
    EPOCH = {}

    def __init__(self, name=""):
        self.name = name
        self.w = dict(Buf.EPOCH)
        self.r = {}


class Prog:
    def __init__(self):
        self.streams = {e: [] for e in ENGS}
        self.cnt = {e: 0 for e in ENGS}
        self.seq = {e: 0 for e in ENGS}
        self.miles = {e: [] for e in ENGS}
        self.waited = {e: {} for e in ENGS}
        self.dma_cnt = {}

    def _resolve(self, tok):
        if tok[0] == 'D':
            return ("d:" + tok[1], tok[2])
        _, eng, seq = tok
        ms = self.miles[eng]
        i = bisect.bisect_left(ms, (seq, -1))
        if i >= len(ms):
            raise RuntimeError(f"unresolved milestone for {tok}")
        return ("e:" + eng, ms[i][1])

    def _wait(self, eng, tok):
        if tok[0] == 'E' and tok[1] == eng and eng == "tensor":
            return
        key, val = self._resolve(tok)
        if self.waited[eng].get(key, 0) >= val:
            return
        self.waited[eng][key] = val
        self.streams[eng].append(("wait", key, val))

    @staticmethod
    def _merge(d, key, tok):
        old = d.get(key)
        if old is None or old[2] < tok[2]:
            d[key] = tok

    def op(self, eng, fn, reads=(), writes=(), inc=True, dsem=None):
        pr = [b for b in reads if b.name.startswith("ps")]
        if pr:
            writes = list(writes) + pr
            reads = [b for b in reads if not b.name.startswith("ps")]
        deps = []
        for b in reads:
            deps.extend(b.w.values())
        for b in writes:
            deps.extend(b.w.values())
            deps.extend(b.r.values())
        for tok in deps:
            self._wait(eng, tok)
        if dsem is not None:
            v = self.dma_cnt.get(dsem, 0) + 16
            self.dma_cnt[dsem] = v
            tok = ('D', dsem, v)
            key = "d:" + dsem
            self.streams[eng].append(("op", fn, key, 16))
        else:
            self.seq[eng] += 1
            tok = ('E', eng, self.seq[eng])
            key = eng
            if inc:
                self.cnt[eng] += 1
                self.miles[eng].append((self.seq[eng], self.cnt[eng]))
                self.streams[eng].append(("op", fn, "e:" + eng, 1))
            else:
                self.streams[eng].append(("op", fn, None, 0))
        for b in reads:
            self._merge(b.r, key, tok)
        for b in writes:
            b.w = {key: tok}
            b.r = {}
        return tok

    def wait_all(self, eng, bufs):
        for b in bufs:
            for tok in list(b.w.values()) + list(b.r.values()):
                self._wait(eng, tok)

    def barrier(self):
        self.marks = getattr(self, "marks", [])
        self.marks.append(dict(self.seq))
        toks = []
        for e in ENGS:
            if self.miles[e]:
                toks.append(("e:" + e, self.miles[e][-1][1]))
        for k, v in self.dma_cnt.items():
            toks.append(("d:" + k, v))
        for e in ENGS:
            for key, val in toks:
                if key == "e:" + e and e == "tensor":
                    continue
                if self.waited[e].get(key, 0) >= val:
                    continue
                self.waited[e][key] = val
                self.streams[e].append(("wait", key, val))

    def epoch(self):
        self.marks = getattr(self, "marks", [])
        self.marks.append(dict(self.seq))
        ep = {}
        for e in ENGS:
            if e == "sync":
                continue
            if self.seq[e] > 0:
                if not (self.miles[e] and self.miles[e][-1][0] == self.seq[e]):
                    print("EPOCH WARNING: last op not a milestone on", e, self.seq[e], self.miles[e][-1] if self.miles[e] else None)
                ep[e] = ('E', e, self.seq[e])
        for k, v in self.dma_cnt.items():
            ep["d:" + k] = ('D', k, v)
        Buf.EPOCH = ep

    def emit(self, nc):
        keys = set()
        for e in ENGS:
            for it in self.streams[e]:
                if it[0] == "wait":
                    keys.add(it[1])
                elif it[2] is not None:
                    keys.add(it[2])
        keys = sorted(keys)
        with ExitStack() as st:
            sems = {k: st.enter_context(nc.semaphore(k.replace(":", "_"))) for k in keys}
            block = st.enter_context(nc.Block())

            def run(ename):
                def body(eng):
                    for it in self.streams[ename]:
                        if it[0] == "wait":
                            eng.wait_ge(sems[it[1]], it[2])
                        else:
                            inst = it[1](eng)
                            if it[2] is not None:
                                inst.then_inc(sems[it[2]], it[3])
                return body

            block.sync(run("sync"))
            block.scalar(run("scalar"))
            block.gpsimd(run("gpsimd"))
            block.vector(run("vector"))
            block.tensor(run("tensor"))


IN_SPECS = dict(
    xp=(S, D), xs=(TS_, D), ck=(4, PAST, 512), cv=(4, PAST, 512), clf=(4, PAST, NH),
    sconv=(4, 30, DC), sconvT=(128, 4, 4, 30), cT=(128, 8, 5),
    w_ada=(D, 6 * D), b_adaT=(128, 48), bada_g5=(5, 2048), n1gT=(128, 8), n2gT=(128, 8),
    w_in=(D, INC), bf_bc=(128, NH), gq_bc=(128, 512), gk_bc=(128, 512),
    conv_wT=(128, 4, CW), conv_bT=(128, 4), ln_gT=(128, 4), ln_bT=(128, 4),
    w_out=(D, D), w_gate=(D, DFF), w_up=(D, DFF), w_down=(DFF, D),
    ident=(128, 128), triU=(128, 128), ones=(128, 128), triLs=(128, 128), triUb=(64, 64),
    maskd=(128, 4, 512), masks=(64, 64), sel_p=(5, 128), sel_s=(5, 128),
)
OUT_SPECS = dict(
    yp=(S, D), ys=(TS_, D), kp=(S, 512), vp=(S, 512), lfp=(S, NH), cvp=(30, DC),
    ksm=(TS_, 512), vsm=(TS_, 512), lfs=(TS_, NH), cvs=(4, 30, DC),
)


def build_program():
    Buf.EPOCH = {}
    nc = bass.Bass("TRN2", target_bir_lowering=False)
    I = {k: nc.dram_tensor(k, list(v), F32, kind="ExternalInput").ap() for k, v in IN_SPECS.items()}
    O = {k: nc.dram_tensor(k, list(v), F32, kind="ExternalOutput").ap() for k, v in OUT_SPECS.items()}
    x1s = nc.dram_tensor("x1_scratch", [TT_, D], F32, kind="Internal").ap()
    P = Prog()
    st = ExitStack()
    ARENA_B = 212736
    arena = st.enter_context(nc.sbuf_tensor("arena", [128, ARENA_B // 4], F32))
    psum = [st.enter_context(nc.psum_tensor(f"ps{i}", [128, 512], F32)) for i in range(8)]
    psb = [Buf(f"ps{i}") for i in range(8)]
    ps_rr = [0]

    def PS():
        i = ps_rr[0] % 8
        ps_rr[0] += 1
        return psum[i], psb[i]

    def PSI(i):
        return psum[i], psb[i]

    top = [0]

    def alloc(shape, dt):
        n = int(np.prod(shape)) * (2 if dt == BF16 else 4)
        n = (n + 63) // 64 * 64
        off = top[0]
        top[0] += n
        assert top[0] <= ARENA_B, f"arena overflow {top[0]}"
        ap = arena[:, off // 4:(off + n) // 4]
        if dt == BF16:
            ap = ap.bitcast(BF16)
        ne = int(np.prod(shape))
        ap = ap[:, 0:ne]
        if len(shape) == 2:
            ap = ap.rearrange("p (a b) -> p a b", a=shape[0])
        elif len(shape) == 3:
            ap = ap.rearrange("p (a b c) -> p a b c", a=shape[0], b=shape[1])
        return ap

    def DMA(eng, out, in_, reads, writes, dsem):
        return P.op(eng, lambda e: e.dma_start(out=out, in_=in_), reads=reads, writes=writes, dsem=dsem)

    def MM(out, lhsT, rhs, start, stop, reads, writes, inc=None):
        if inc is None:
            inc = stop
        return P.op("tensor", lambda e: e.matmul(out=out, lhsT=lhsT, rhs=rhs, start=start, stop=stop,
                                                 skip_group_check=True),
                    reads=reads, writes=writes, inc=inc)

    def TR(out, in_, ident_ap, reads, writes, inc=True):
        return P.op("tensor", lambda e: e.transpose(out=out, in_=in_, identity=ident_ap),
                    reads=reads, writes=writes, inc=inc)

    def ACT(out, in_, func, reads, writes, scale=None, bias=None, accum=None):
        kw = {}
        if scale is not None:
            kw["scale"] = scale
        if bias is not None:
            kw["bias"] = bias
        if accum is not None:
            kw["accum_out"] = accum
        return P.op("scalar", lambda e: e.activation(out=out, in_=in_, func=func, **kw), reads=reads, writes=writes)

    def TSC(eng, out, in0, s1, s2, op0, op1, reads, writes):
        if s2 is None:
            return P.op(eng, lambda e: e.tensor_scalar(out=out, in0=in0, scalar1=s1, scalar2=None, op0=op0),
                        reads=reads, writes=writes)
        return P.op(eng, lambda e: e.tensor_scalar(out=out, in0=in0, scalar1=s1, scalar2=s2, op0=op0, op1=op1),
                    reads=reads, writes=writes)

    def TT(eng, out, in0, in1, op, reads, writes):
        return P.op(eng, lambda e: e.tensor_tensor(out=out, in0=in0, in1=in1, op=op), reads=reads, writes=writes)

    def STT(out, in0, scalar, in1, op0, op1, reads, writes):
        return P.op("vector", lambda e: e.scalar_tensor_tensor(out=out, in0=in0, scalar=scalar, in1=in1,
                                                               op0=op0, op1=op1), reads=reads, writes=writes)

    def CP(eng, out, in_, reads, writes):
        if eng == "scalar":
            return P.op(eng, lambda e: e.copy(out=out, in_=in_), reads=reads, writes=writes)
        return P.op(eng, lambda e: e.tensor_copy(out=out, in_=in_), reads=reads, writes=writes)

    def MSET(eng, ap, val, writes):
        return P.op(eng, lambda e: e.memset(ap, val), writes=writes)

    def RED(out, in_, reads, writes):
        return P.op("vector", lambda e: e.tensor_reduce(out=out, in_=in_, axis=AX.X, op=ALU.add),
                    reads=reads, writes=writes)

    cst = Buf("cst")
    C = {}

    cl_cnt = [0]

    def cload(name, shape, src=None, eng="sync", dt=F32):
        t = alloc(shape, dt)
        C[name] = t
        if eng == "sync":
            cl_cnt[0] += 1
            q_ = "sync" if cl_cnt[0] % 2 == 0 else "scalar"
            DMA(q_, t, I[name] if src is None else src, [], [Buf()], "cst" if q_ == "sync" else "csta")
        else:
            DMA(eng, t, I[name] if src is None else src, [], [Buf()], "cstg")
        return t

    ident = cload("ident", (128,))
    triU = cload("triU", (128,))
    ones = cload("ones", (128,))
    triLs = cload("triLs", (128,))
    cT = cload("cT", (8, 5))
    b_adaT = cload("b_adaT", (48,))
    n1gT = cload("n1gT", (8,))
    n2gT = cload("n2gT", (8,))
    bf_bc = cload("bf_bc", (NH,))
    gq_bc = cload("gq_bc", (512,))
    gk_bc = cload("gk_bc", (512,))
    conv_wT = cload("conv_wT", (4, CW))
    conv_bT = cload("conv_bT", (4,))
    ln_gT = cload("ln_gT", (4,))
    ln_bT = cload("ln_bT", (4,))
    nln_g = alloc((4,), F32)
    nln_b = alloc((4,), F32)
    identb = cload("identb", (128,), src=I["ident"], eng="gpsimd", dt=BF16)
    onesb = cload("onesb", (128,), src=I["ones"], eng="gpsimd", dt=BF16)
    triUb = alloc((64,), F32)
    DMA("sync", triUb[0:64], I["triUb"], [], [Buf()], "cst")
    masks = alloc((64,), F32)
    DMA("sync", masks[0:64], I["masks"], [], [Buf()], "cst")
    sel_p = alloc((128,), F32)
    DMA("sync", sel_p[0:5], I["sel_p"], [], [Buf()], "cst")
    sel_s = alloc((128,), F32)
    DMA("sync", sel_s[0:5], I["sel_s"], [], [Buf()], "cst")
    modTM = alloc((2048,), F32)
    negh = alloc((512,), F32)
    MSET("gpsimd", negh, -0.5, [cst])
    eps_t = alloc((1,), F32)
    MSET("gpsimd", eps_t, EPS, [cst])
    P.barrier()
    TSC("vector", nln_g, ln_gT, -1.0, None, ALU.mult, None, [cst], [cst])
    TSC("vector", nln_b, ln_bT, -1.0, None, ALU.mult, None, [cst], [cst])
    modFM = alloc((4, 8, 5), F32)
    A1 = alloc((8, 5), F32)
    A2 = alloc((8, 5), F32)
    siluT = alloc((8, 8), BF16)
    sgc = alloc((8, 5), F32)
    logf_all = alloc((17, NH), F32)
    qTs = alloc((4, 64), BF16)
    kTs = alloc((4, 64), BF16)
    Vnew = alloc((NH, 65), BF16)
    catC = alloc((4, TT_), BF16)
    hT = alloc((8, TT_), BF16)
    hTb = [[Buf() for _ in range(8)] for _ in range(17)]
    catb = [[Buf() for _ in range(5)] for _ in range(8)]
    PH = top[0]
    print("persistent bytes", PH)

    SUB = float(os.environ.get('MK_SUB', '9'))
    if SUB == 0:
        return finish(nc, P, st, O, None, None, None, None)
    modb = Buf("mod")
    MSET("vector", siluT, 0.0, [modb])
    ACT(sgc, cT, AF.Exp, [cst], [modb], scale=-1.0)
    TSC("vector", sgc, sgc, 1.0, None, ALU.add, None, [modb], [modb])
    ACT(sgc, sgc, AF.Ln, [modb], [modb])
    ACT(sgc, sgc, AF.Exp, [modb], [modb], scale=-1.0)
    TT("vector", siluT[:, :, 0:5], cT, sgc, ALU.mult, [modb], [modb])
    if SUB == 1:
        return finish(nc, P, st, O, None, None, None, None)
    arena_end = ARENA_B - 64
    save_top = top[0]
    MOD_LO = arena_end - (2 * 8 * 256 * 4 + 2 * 8 * 256 * 2 + 2048 * 4)
    top[0] = MOD_LO
    wa32 = [alloc((8, 256), F32) for _ in range(2)]
    wa16 = [alloc((8, 256), BF16) for _ in range(2)]
    wa16b = [Buf() for _ in range(2)]
    bada5 = alloc((2048,), F32)
    top[0] = save_top
    DMA("sync", bada5[0:5], I["bada_g5"], [], [Buf()], "bada5")
    wab = [Buf() for _ in range(2)]
    w_ada_v = I["w_ada"].rearrange("(k p) n -> p k n", p=128)
    modfb = Buf("modFM")
    modtb = Buf("modTM")
    P.barrier()

    def mod_dma(g):
        sl_ = g % 2
        DMA("gpsimd", wa32[sl_], w_ada_v[:, :, g * 256:(g + 1) * 256], [], [wab[sl_]], f"wa{sl_}")

    def mod_cast(g):
        sl_ = g % 2
        CP("vector", wa16[sl_], wa32[sl_], [wab[sl_]], [wa16b[sl_]])

    def mod_mm(g):
        sl_ = g % 2
        ps, pb = PS()
        kindg = g // 4
        if kindg in (0, 1, 3, 4):
            kind = {0: 0, 1: 1, 3: 2, 4: 3}[kindg]
            for cc in range(2):
                for kc in range(8):
                    MM(ps[:, cc * 8:cc * 8 + 8], wa16[sl_][:, kc, cc * 128:(cc + 1) * 128], siluT[:, kc, :],
                       kc == 0, kc == 7, [wa16b[sl_], modb], [pb])
            for cc in range(2):
                gc = g * 2 + cc
                TSC("vector", modFM[:, kind, (g % 4) * 2 + cc, :], ps[:, cc * 8:cc * 8 + 5],
                    b_adaT[:, gc:gc + 1], None, ALU.add, None, [pb], [modfb])
        else:
            for kc in range(8):
                MM(ps[0:5, 0:256], siluT[:, kc, 0:5], wa16[sl_][:, kc, :], kc == 0, kc == 7, [wa16b[sl_], modb], [pb])
            tcol = (g % 4) * 256 + (0 if kindg == 2 else 1024)
            TT("vector", modTM[0:5, tcol:tcol + 256], ps[0:5, 0:256], bada5[0:5, tcol:tcol + 256],
               ALU.add, [pb], [modtb])

    mod_dma(0)
    mod_dma(1)
    for g in range(8):
        mod_cast(g)
        mod_mm(g)
        mod_dma(g + 2)
    mod_cast(8)
    mod_cast(9)
    mod_dma(10)
    mod_dma(11)
    STT(A1, modFM[:, 1], 1.0, n1gT.unsqueeze(2).to_broadcast([128, 8, 5]), ALU.add, ALU.mult, [modfb], [modfb])
    B1 = modFM[:, 0]
    B2 = modFM[:, 2]
    if STAGE == 0:
        return finish(nc, P, st, O, None, None, None, None)

    w_in_v = I["w_in"].rearrange("(k p) n -> p k n", p=128)
    top[0] = PH
    slot0 = top[0]
    xt = [alloc((D,), F32) for _ in range(2)]
    xtb = [Buf() for _ in range(2)]
    xn = [alloc((D,), F32) for _ in range(2)]
    xnb = [Buf() for _ in range(2)]
    junk = alloc((D,), BF16)
    junkb = Buf()
    sgt = [alloc((512,), F32) for _ in range(2)]
    sgb = [Buf() for _ in range(2)]
    top[0] = max(top[0], slot0 + 8 * 1544 * 2 + 64)
    uT = alloc((4, 30 + S), BF16)
    uTs = alloc((4, 4, 46), BF16)
    uTb = [[Buf() for _ in range(5)] for _ in range(4)]
    diag = alloc((4, CW, 128), BF16)
    diagb = Buf()
    stat = alloc((17, 4), F32)
    statb = [Buf() for _ in range(17)]
    ovl = top[0]
    w_in_u = alloc((8, 1024), BF16)
    wub = Buf()
    for hh in range(2):
        DMA("gpsimd", w_in_u[:, :, hh * 512:(hh + 1) * 512], w_in_v[:, :, hh * 512:(hh + 1) * 512], [], [wub],
            "w_in_u")
    utm = alloc((512,), F32)
    utmb = Buf()
    scrb = [Buf() for _ in range(4)]
    sconv_sb = alloc((4, 4, 30), F32)
    scb = Buf()
    DMA("sync", sconv_sb, I["sconvT"], [], [scb], "sconv")
    p1_top = top[0]
    top[0] = ovl
    yf = alloc((4, 512), F32)
    yfb = [Buf() for _ in range(4)]
    ybf = alloc((4, 512), BF16)
    ysq = alloc((4, 512), BF16)
    ybb = [Buf() for _ in range(4)]
    mean_t = alloc((512,), F32)
    msq_t = alloc((512,), F32)
    rstd_t = alloc((512,), F32)
    stb = Buf()
    dtmp = alloc((512,), F32)
    vtmp = alloc((512,), F32)
    stmp = alloc((512,), F32)
    dtb = Buf()
    top[0] = max(top[0], p1_top)
    print("P1 top", top[0], "MOD_LO", MOD_LO)
    assert top[0] <= MOD_LO

    def build_diag(c):
        for j in range(CW):
            TSC("vector", diag[:, c, j, :], identb, conv_wT[:, c, j:j + 1], None, ALU.mult, None, [cst], [diagb])
    for c in range(4):
        MSET("vector", uT[:, c, 0:30], 0.0, [uTb[c][0]])
        CP("vector", uTs[:, c, :, 0:30], sconv_sb[:, c, :, :], [scb], [uTb[c][4]])

    def tile_rows(i):
        return (128, 128 * i) if i < 16 else (64, S)

    nb = {}

    def norm_A(i, slot):
        R, tok0 = tile_rows(i)
        xt, xtb, xn, xnb, junk, junkb, stat, statb = (nb[k] for k in
                                                      ("xt", "xtb", "xn", "xnb", "junk", "junkb", "stat", "statb"))
        ACT(junk[0:R], xt[slot][0:R], AF.Square, [xtb[slot]], [junkb, statb[i]], accum=stat[0:R, i, 0:1])
        ACT(stat[0:R, i, 1:2], stat[0:R, i, 0:1], AF.Ln, [statb[i], cst], [statb[i]], scale=1.0 / D, bias=eps_t[0:R])
        ACT(stat[0:R, i, 2:3], stat[0:R, i, 1:2], AF.Exp, [statb[i]], [statb[i]], scale=-0.5)
        nx = len(xn)
        ACT(xn[i % nx][0:R], xt[slot][0:R], AF.Identity, [xtb[slot], statb[i]], [xnb[i % nx]], scale=stat[0:R, i, 2:3])

    def norm_B(i, gA, gB, dst_hT, dst_bufs):
        R, tok0 = tile_rows(i)
        nx = len(nb["xn"])
        xn, xnb = nb["xn"][i % nx], nb["xnb"][i % nx]
        for half in range(2):
            ps, pb = PS()
            for q in range(4):
                kc = half * 4 + q
                TR(ps[:, q * 128:q * 128 + R], xn[0:R, kc * 128:(kc + 1) * 128], ident[0:R, 0:R], [xnb], [pb],
                   inc=(q == 3))
            for q in range(4):
                kc = half * 4 + q
                if i < 16:
                    groups = [(0, R, 0)]
                else:
                    groups = [(b * 16, 16, 1 + b) for b in range(4)]
                for (c0, n, row) in groups:
                    o = dst_hT[:, kc, tok0 + c0:tok0 + c0 + n]
                    src_ps = ps[:, q * 128 + c0:q * 128 + c0 + n]
                    if half == 0:
                        ACT(o, src_ps, AF.Identity, [pb, modfb], [dst_bufs[i][kc]],
                            scale=gA[:, kc, row:row + 1], bias=gB[:, kc, row:row + 1])
                    else:
                        STT(o, src_ps, gA[:, kc, row:row + 1], gB[:, kc, row:row + 1].to_broadcast([128, n]),
                            ALU.mult, ALU.add, [pb, modfb], [dst_bufs[i][kc]])

    glu_cnt = [0]

    def glu_block(pa, pab, pg, pgb, out_ap, out_buf, n, rows=128, fp32_out=None):
        glu_cnt[0] += 1
        s = glu_cnt[0] % 2
        ACT(sgt[s][0:rows, 0:n], pg, AF.Exp, [pgb], [sgb[s]], scale=-1.0)
        ACT(sgt[s][0:rows, 0:n], sgt[s][0:rows, 0:n], AF.Ln, [sgb[s], cst], [sgb[s]], bias=ones[0:rows, 0:1])
        ACT(sgt[s][0:rows, 0:n], sgt[s][0:rows, 0:n], AF.Exp, [sgb[s]], [sgb[s]], scale=-1.0)
        TT("vector", out_ap, pa, sgt[s][0:rows, 0:n] if out_ap.ndim == 2 else
           sgt[s][0:rows, 0:n].rearrange("p (b t) -> p b t", b=4), ALU.mult, [pab, sgb[s]], [out_buf])

    nb.update(xt=xt, xtb=xtb, xn=xn, xnb=xnb, junk=junk, junkb=junkb, stat=stat, statb=statb)
    xp_v = I["xp"]
    if SUB == 10:
        return finish(nc, P, st, O, None, None, None, None)
    def p1_post(i):
        R, tok0 = tile_rows(i)
        if i % 2 == 1 and i < 16:
            g0_ = 8 + (i - 1)
            for g_ in (g0_, g0_ + 1):
                mod_mm(g_)
            for g_ in (g0_ + 2, g0_ + 3):
                if g_ < 24:
                    mod_cast(g_)
            for g_ in (g0_ + 4, g0_ + 5):
                if g_ < 24:
                    mod_dma(g_)
            if g0_ + 1 == 19:
                STT(A2, modFM[:, 3], 1.0, n2gT.unsqueeze(2).to_broadcast([128, 8, 5]), ALU.add, ALU.mult,
                    [modfb], [modfb])
        if SUB == 12:
            return
        if i % 4 == 3 or i == 16:
            j = i // 4
            tiles = list(range(4 * j, 4 * j + 4)) if i < 16 else [16]
            n = 512 if i < 16 else 64
            t0 = 512 * j if i < 16 else S
            for c in range(4):
                pa, pab = PS()
                pg, pgb = PS()
                for (pp, ppb, coff) in ((pa, pab, 0), (pg, pgb, 512)):
                    for kc in range(8):
                        MM(pp[:, 0:n], w_in_u[:, kc, coff + c * 128:coff + (c + 1) * 128], hT[:, kc, t0:t0 + n],
                           kc == 0, kc == 7, [wub] + [hTb[t][kc] for t in tiles], [ppb])
                if i < 16:
                    glu_block(pa[:, 0:n], pab, pg[:, 0:n], pgb, uT[:, c, 30 + t0:30 + t0 + n], uTb[c][j], n)
                else:
                    glu_block(pa[:, 0:n].rearrange("p (b t) -> p b t", b=4), pab, pg[:, 0:n], pgb,
                              uTs[:, c, :, 30:46], uTb[c][4], n)
        if SUB == 13:
            return
        if i == 15 or i == 16:
            m0, M = (S - 32, 32) if i == 15 else (S, 64)
            pa, pab = PS()
            pg, pgb = PS()
            for (pp, ppb, coff) in ((pa, pab, 0), (pg, pgb, 512)):
                for kc in range(8):
                    MM(pp[0:M, :], hT[:, kc, m0:m0 + M], w_in_u[:, kc, coff:coff + 512], kc == 0, kc == 7,
                       [wub, hTb[i][kc]], [ppb])
            glu_block(pa[0:M, :], pab, pg[0:M, :], pgb, utm[0:M, 0:512], utmb, 512, rows=M)
            if i == 15:
                DMA("sync", O["cvp"], utm[2:32, 0:512], [utmb], [Buf()], "cvoutp")
            else:
                for b in range(4):
                    DMA("sync", O["cvs"][b, 14:30, :], utm[16 * b:16 * b + 16, 0:512], [utmb], [Buf()], f"cvouta{b}")
                    DMA("sync", utm[64 + 14 * b:64 + 14 * b + 14, 0:512], I["sconv"][b, 16:30, :], [], [scrb[b]], f"scr_in{b}")
                    DMA("sync", O["cvs"][b, 0:14, :], utm[64 + 14 * b:64 + 14 * b + 14, 0:512], [scrb[b]], [Buf()], f"cvoutb{b}")


    for i in range(17):
        R, tok0 = tile_rows(i)
        slot = i % 2
        src = xp_v[tok0:tok0 + R, :] if i < 16 else I["xs"]
        DMA("sync", xt[slot][0:R], src, [], [xtb[slot]], f"xt{slot}")
        norm_A(i, slot)
        if i > 0:
            norm_B(i - 1, A1, B1, hT, hTb)
        if i > 1:
            p1_post(i - 2)
        if i in (9, 11, 13, 15):
            build_diag((i - 9) // 2)
    norm_B(16, A1, B1, hT, hTb)
    p1_post(15)
    p1_post(16)
    if STAGE == 1:
        return finish(nc, P, st, O, None, None, None, None)
    P.epoch()
    top_save = top[0]
    top[0] = slot0
    w_qkv = alloc((8, 1544), BF16)
    wqb = Buf()
    for hh in range(3):
        DMA("gpsimd", w_qkv[:, :, hh * 512:(hh + 1) * 512], w_in_v[:, :, 1024 + hh * 512:1024 + (hh + 1) * 512], [],
            [wqb], "w_qkv")
    DMA("gpsimd", w_qkv[:, :, 1536:1544], w_in_v[:, :, 2560:2568], [], [wqb], "w_qkv")
    top[0] = top_save

    yfb = [Buf() for _ in range(4)]
    ybb = [Buf() for _ in range(4)]
    stb = Buf()
    dtb = Buf()
    for j in (4, 0, 1, 2, 3):
        n = 512 if j < 4 else 64
        t0 = 512 * j if j < 4 else S
        for c in range(4):
            py, pyb = PS()
            for tap in range(CW):
                if j < 4:
                    rhs = uT[:, c, t0 + tap:t0 + tap + n]
                    o = py[:, 0:n]
                    rb = [uTb[c][jj] for jj in range(max(0, j - 1), j + 1)]
                else:
                    rhs = uTs[:, c, :, tap:tap + 16]
                    o = py[:, 0:n].rearrange("p (b t) -> p b t", b=4)
                    rb = [uTb[c][4]]
                MM(o, diag[:, c, tap, :], rhs, tap == 0, tap == CW - 1, [diagb] + rb, [pyb])
            ACT(yf[:, c, 0:n], py[:, 0:n], AF.Identity, [pyb, cst], [yfb[c]], bias=conv_bT[:, c:c + 1])
            CP("vector", ybf[:, c, 0:n], yf[:, c, 0:n], [yfb[c]], [ybb[c]])
            ACT(ysq[:, c, 0:n], yf[:, c, 0:n], AF.Square, [yfb[c]], [ybb[c]])
        p1, p1b = PS()
        p2, p2b = PS()
        for c in range(4):
            MM(p1[:, 0:n], onesb, ybf[:, c, 0:n], c == 0, c == 3, [ybb[c], cst], [p1b])
        for c in range(4):
            MM(p2[:, 0:n], onesb, ysq[:, c, 0:n], c == 0, c == 3, [ybb[c], cst], [p2b])
        TSC("vector", mean_t[:, 0:n], p1[:, 0:n], 1.0 / DC, None, ALU.mult, None, [p1b], [stb])
        TT("vector", msq_t[:, 0:n], mean_t[:, 0:n], mean_t[:, 0:n], ALU.mult, [stb], [stb])
        STT(msq_t[:, 0:n], p2[:, 0:n], 1.0 / DC, msq_t[:, 0:n], ALU.mult, ALU.subtract, [p2b, stb], [stb])
        TSC("vector", msq_t[:, 0:n], msq_t[:, 0:n], EPS, None, ALU.add, None, [stb], [stb])
        ACT(rstd_t[:, 0:n], msq_t[:, 0:n], AF.Ln, [stb], [stb])
        ACT(rstd_t[:, 0:n], rstd_t[:, 0:n], AF.Exp, [stb], [stb], scale=-0.5)
        for c in range(4):
            TT("vector", dtmp[:, 0:n], yf[:, c, 0:n], mean_t[:, 0:n], ALU.subtract, [yfb[c], stb], [dtb])
            TT("vector", dtmp[:, 0:n], dtmp[:, 0:n], rstd_t[:, 0:n], ALU.mult, [dtb, stb], [dtb])
            ACT(stmp[:, 0:n], dtmp[:, 0:n], AF.Exp, [dtb, cst], [dtb], scale=nln_g[:, c:c + 1],
                bias=nln_b[:, c:c + 1])
            ACT(stmp[:, 0:n], stmp[:, 0:n], AF.Ln, [dtb, cst], [dtb], bias=ones[:, 0:1])
            ACT(stmp[:, 0:n], stmp[:, 0:n], AF.Exp, [dtb], [dtb], scale=-1.0)
            ACT(vtmp[:, 0:n], dtmp[:, 0:n], AF.Identity, [dtb, cst], [dtb], scale=ln_gT[:, c:c + 1],
                bias=ln_bT[:, c:c + 1])
            TT("vector", catC[:, c, t0:t0 + n], vtmp[:, 0:n], stmp[:, 0:n], ALU.mult, [dtb], [catb[c][j]])

    if STAGE == 1.5:
        return finish(nc, P, st, O, None, None, None, None)
    P.epoch()
    top[0] = slot0 + 8 * 1544 * 2 + 64
    Vaug = alloc((NT, NH, 65), BF16)
    vab = [Buf() for _ in range(NT)]
    qT = alloc((NH, S), BF16)
    kT = alloc((NH, S), BF16)
    qTb = [[Buf() for _ in range(NT)] for _ in range(NH)]
    kTb = [[Buf() for _ in range(NT)] for _ in range(NH)]
    P2S = top[0]
    Qs2 = [alloc((NH, 70), F32) for _ in range(2)]
    Ks2 = [alloc((NH, 70), F32) for _ in range(2)]
    qsb2 = [Buf(), Buf()]
    ksb2 = [Buf(), Buf()]
    kout = alloc((512,), F32)
    vout = alloc((512,), F32)
    koutb, voutb = Buf(), Buf()
    sqt = alloc((512,), F32)
    tq = alloc((512,), F32)
    sqb = Buf()
    nstat = alloc((3, NH), F32)
    nsb = Buf()
    zt = alloc((17, NH), F32)
    et = alloc((17, NH), F32)
    spre = alloc((16, NH), F32)
    cum = alloc((16, NH), F32)
    hb = alloc((16 * NH,), BF16)
    h32 = alloc((16 * NH,), F32)
    r1 = alloc((16 * NH,), F32)
    augQ = alloc((16, NH, 6), F32)
    augK = alloc((16, NH, 6), F32)
    lb = Buf("logf")
    augb = Buf("aug")
    Qss = alloc((512,), F32)
    Kss = alloc((512,), F32)
    sab = Buf("sample_misc")
    print("P2 top", top[0])

    pf, pfb = PS()
    for i in range(17):
        R, tok0 = tile_rows(i)
        for kc in range(8):
            MM(pf[0:R, i * 8:(i + 1) * 8], hT[:, kc, tok0:tok0 + R], w_qkv[:, kc, 1536:1544], kc == 0, kc == 7,
               [wqb, hTb[i][kc]], [pfb])
    for (r0, r1_, i0, i1) in ((0, 128, 0, 16), (0, 64, 16, 17)):
        ni = i1 - i0
        TT("vector", zt[r0:r1_, i0:i1, :], pf[r0:r1_, i0 * 8:i1 * 8].rearrange("p (i h) -> p i h", h=NH),
           bf_bc[r0:r1_].unsqueeze(1).to_broadcast([r1_ - r0, ni, NH]), ALU.add, [pfb, cst], [lb])
        ACT(et[r0:r1_, i0:i1, :], zt[r0:r1_, i0:i1, :], AF.Exp, [lb], [lb], scale=-1.0)
        TSC("vector", et[r0:r1_, i0:i1, :], et[r0:r1_, i0:i1, :], 1.0, None, ALU.add, None, [lb], [lb])
        ACT(zt[r0:r1_, i0:i1, :], et[r0:r1_, i0:i1, :], AF.Ln, [lb], [lb])
        TSC("vector", logf_all[r0:r1_, i0:i1, :], zt[r0:r1_, i0:i1, :], -1.0, None, ALU.mult, None, [lb], [lb])
    DMA("sync", O["lfp"].rearrange("(i p) h -> p i h", p=128), logf_all[:, 0:16, :], [lb], [Buf()], "lfout")
    DMA("sync", O["lfs"], logf_all[0:64, 16, :], [lb], [Buf()], "lfout")
    MSET("vector", spre[:, 0, :], 0.0, [lb])
    for i in range(1, 16):
        TT("vector", spre[:, i, :], spre[:, i - 1, :], logf_all[:, i - 1, :], ALU.add, [lb], [lb])
    pc, pcb = PS()
    for i in range(16):
        MM(pc[:, i * 8:(i + 1) * 8], triU, logf_all[:, i, :], True, i == 0, [lb, cst], [pcb], inc=(i == 0))
        if i > 0:
            MM(pc[:, i * 8:(i + 1) * 8], ones, spre[:, i, :], False, True, [lb, cst], [pcb])
    cumf = cum.rearrange("p i h -> p (i h)")
    CP("vector", cumf, pc[:, 0:128], [pcb], [augb])
    MSET("gpsimd", augQ, 1.0, [augb])
    MSET("gpsimd", augK, 1.0, [augb])
    aQ = augQ.rearrange("p i h r -> p (i h) r")
    aK = augK.rearrange("p i h r -> p (i h) r")
    cur = cumf
    for lvl in range(3):
        CP("vector", hb, cur, [augb], [augb])
        CP("vector", h32, hb, [augb], [augb])
        CP("vector", aQ[:, :, lvl], h32, [augb], [augb])
        TSC("vector", aK[:, :, 3 + lvl], h32, -1.0, None, ALU.mult, None, [augb], [augb])
        if lvl < 2:
            TT("vector", r1, cur, h32, ALU.subtract, [augb], [augb])
            cur = r1

    for i in range(NT):
        MSET("gpsimd", Vaug[:, i, :, 64:65], 1.0, [vab[i]])
    MSET("gpsimd", Vnew[:, :, 64:65], 1.0, [sab])

    def qk_norm(psrc, pb, R, g_bc, pre_scale, out_ap, out_buf):
        ACT(sqt[0:R], psrc[0:R], AF.Square, [pb], [sqb])
        RED(nstat[0:R, 0, :], sqt[0:R].rearrange("p (h d) -> p h d", h=NH), [sqb], [nsb])
        ACT(nstat[0:R, 1, :], nstat[0:R, 0, :], AF.Ln, [nsb, cst], [nsb], scale=1.0 / HD, bias=eps_t[0:R])
        ACT(nstat[0:R, 2, :], nstat[0:R, 1, :], AF.Exp, [nsb], [nsb], scale=-0.5)
        TT("vector", tq[0:R].rearrange("p (h d) -> p h d", h=NH), psrc[0:R].rearrange("p (h d) -> p h d", h=NH),
           nstat[0:R, 2, :].unsqueeze(2).to_broadcast([R, NH, HD]), ALU.mult, [pb, nsb], [sqb])
        gv = g_bc[0:R] if out_ap.ndim == 2 else g_bc[0:R].rearrange("p (h d) -> p h d", h=NH)
        tv = tq[0:R] if out_ap.ndim == 2 else tq[0:R].rearrange("p (h d) -> p h d", h=NH)
        STT(out_ap, tv, pre_scale, gv, ALU.mult, ALU.mult, [sqb, cst], [out_buf])

    p2ps = {}
    trc = [0]

    def p2_mm(i):
        R, tok0 = tile_rows(i)
        pq, pqb = PS()
        pk, pkb = PS()
        pv, pvb = PS()
        p2ps[i] = (pq, pqb, pk, pkb, pv, pvb)
        for (pp, ppb, coff) in ((pq, pqb, 0), (pk, pkb, 512), (pv, pvb, 1024)):
            for kc in range(8):
                MM(pp[0:R, :], hT[:, kc, tok0:tok0 + R], w_qkv[:, kc, coff:coff + 512], kc == 0, kc == 7,
                   [wqb, hTb[i][kc]], [ppb])

    def p2_chain(i):
        R, tok0 = tile_rows(i)
        pq, pqb, pk, pkb, pv, pvb = p2ps[i]
        Qs, Ks, qsb, ksb = Qs2[i % 2], Ks2[i % 2], qsb2[i % 2], ksb2[i % 2]
        if i < 16:
            qk_norm(pq, pqb, R, gq_bc, HD ** -0.5, Qs[:, :, 0:64], qsb)
            CP("gpsimd", Qs[:, :, 64:70], augQ[:, i], [augb], [qsb])
            qk_norm(pk, pkb, R, gk_bc, 1.0, kout[0:R], koutb)
            DMA("sync", O["kp"][tok0:tok0 + R, :], kout[0:R], [koutb], [Buf()], "kout")
            CP("scalar", Ks[:, :, 0:64], kout.rearrange("p (h d) -> p h d", h=NH), [koutb], [ksb])
            CP("gpsimd", Ks[:, :, 64:70], augK[:, i], [augb], [ksb])
            CP("scalar", vout[0:R], pv[0:R], [pvb], [voutb])
            DMA("sync", O["vp"][tok0:tok0 + R, :], vout[0:R], [voutb], [Buf()], "vout")
            CP("vector", Vaug[:, i, :, 0:64], vout.rearrange("p (h d) -> p h d", h=NH), [voutb], [vab[i]])
        else:
            qk_norm(pq, pqb, R, gq_bc, HD ** -0.5, Qss[0:R], sab)
            qk_norm(pk, pkb, R, gk_bc, 1.0, Kss[0:R], sab)
            DMA("sync", O["ksm"], Kss[0:R], [sab], [Buf()], "ksout")
            CP("scalar", vout[0:R], pv[0:R], [pvb], [voutb])
            DMA("sync", O["vsm"], vout[0:R], [voutb], [Buf()], "vout")
            CP("vector", Vnew[0:R, :, 0:64], vout[0:R].rearrange("p (h d) -> p h d", h=NH), [voutb], [sab])

    def p2_tr(i):
        R, tok0 = tile_rows(i)
        Qs, Ks, qsb, ksb = Qs2[i % 2], Ks2[i % 2], qsb2[i % 2], ksb2[i % 2]
        if i < 16:
            for (src_s, srcb, dstT, dstb) in ((Qs, qsb, qT, qTb), (Ks, ksb, kT, kTb)):
                for hg in range(2):
                    ps, pb = PS()
                    for hq in range(4):
                        h = hg * 4 + hq
                        TR(ps[0:70, hq * 128:(hq + 1) * 128], src_s[:, h, :], ident, [srcb, cst], [pb], inc=(hq == 3))
                    eng = "scalar" if hg == 0 else "vector"
                    CP(eng, dstT[0:70, hg * 4:hg * 4 + 4, tok0:tok0 + 128],
                       ps[0:70, :].rearrange("p (h t) -> p h t", h=4), [pb], [dstb[hg * 4 + q_][i] for q_ in range(4)])
        else:
            for (src_s, dstT) in ((Qss, qTs), (Kss, kTs)):
                ps, pb = PS()
                for pr in range(4):
                    TR(ps[:, pr * 64:(pr + 1) * 64], src_s[0:R, pr * 128:(pr + 1) * 128], ident[0:R, 0:R], [sab, cst],
                       [pb], inc=(pr == 3))
                CP("vector", dstT, ps[:, 0:256].rearrange("p (a t) -> p a t", a=4), [pb], [sab])

    for i in range(17):
        p2_mm(i)
        p2_chain(i)
        if i > 0:
            p2_tr(i - 1)
    p2_tr(16)

    if STAGE < 3:
        return finish(nc, P, st, O, None, None, None, None)

    P.epoch()
    top[0] = slot0
    catA = alloc((4, TT_), BF16)
    catAb = [[Buf() for _ in range(5)] for _ in range(4)]
    maskd = alloc((4, 512), BF16)
    mkb = Buf()
    DMA("gpsimd", maskd, I["maskd"], [], [mkb], "maskd")
    assert top[0] <= slot0 + 8 * 1544 * 2 + 64
    top[0] = P2S
    w_out_v = I["w_out"].rearrange("(k p) n -> p k n", p=128)
    w_o = alloc((8, 1024), BF16)
    wob = Buf()
    for hh in range(2):
        DMA("gpsimd", w_o[:, :, hh * 512:(hh + 1) * 512], w_out_v[:, :, hh * 512:(hh + 1) * 512], [], [wob], "w_out")
    PT = [alloc((512,), BF16) for _ in range(4)]
    PTb = [Buf() for _ in range(4)]
    Rbc = [alloc((512,), F32) for _ in range(2)]
    Rbb = [Buf() for _ in range(2)]
    rsrow = alloc((512,), F32)
    rsb = Buf()
    lnr = alloc((512,), F32)
    lnb = Buf()
    clf_sb = alloc((4, 8, NH), F32)
    clfb = Buf()
    for b in range(4):
        DMA("sync", clf_sb[:, b], I["clf"][b].rearrange("(c p) h -> p c h", p=128), [], [clfb], "clf")
    P3S = top[0]
    print("P3 top", top[0])

    def normalize(po, pob, ncols, out_fn, slot):
        CP("vector", rsrow[64:65, 0:ncols], po[64:65, 0:ncols], [pob], [rsb])
        pbc, pbcb = PSI(6 + slot)
        MM(pbc[0:64, 0:ncols], ones[64:65, 0:64], rsrow[64:65, 0:ncols], True, True, [rsb, cst], [pbcb])
        P.op("vector", lambda e: e.reciprocal(out=Rbc[slot][0:64, 0:ncols], in_=pbc[0:64, 0:ncols]),
             reads=[pbcb], writes=[Rbb[slot]])
        out_fn(Rbc[slot], Rbb[slot])

    its = []
    hj = 0
    for h in range(NH):
        for j in range(4):
            nkc = 4 * j + 4
            for c in range(nkc):
                its.append((h, j, c, nkc, hj))
            hj += 1

    def emit_qk(k):
        h, j, c, nkc, hj_ = its[k]
        r = c - 4 * j
        q0 = 128 * r if r > 0 else 0
        pss, pssb = PSI(k % 4)
        qb = [qTb[h][t] for t in range(4 * j + q0 // 128, 4 * j + 4)]
        MM(pss[:, q0:512], kT[0:70, h, c * 128:(c + 1) * 128], qT[0:70, h, 512 * j + q0:512 * j + 512],
           True, r < 0, [kTb[h][c]] + qb, [pssb])
        if r >= 0:
            MM(pss[:, q0:q0 + 128], identb, maskd[:, r, q0:q0 + 128], False, True, [mkb, cst], [pssb])
        ACT(PT[k % 4][:, q0:512], pss[:, q0:512], AF.Exp, [pssb], [PTb[k % 4]])

    def emit_pv(k):
        h, j, c, nkc, hj_ = its[k]
        r = c - 4 * j
        q0 = 128 * r if r > 0 else 0
        po, pob = PSI(4 + hj_ % 2)
        MM(po[0:65, q0:512], Vaug[:, c, h, :], PT[k % 4][:, q0:512], c == 0, c == nkc - 1, [vab[c], PTb[k % 4]],
           [pob])
        if c == nkc - 1:
            def outp(R_, Rb_, h=h, j=j, po=po, pob=pob):
                TT("vector", catA[64 * (h % 2):64 * (h % 2) + 64, h // 2, 512 * j:512 * j + 512], po[0:64, :],
                   R_[0:64, :], ALU.mult, [pob, Rb_], [catAb[h // 2][j]])
            pending.append((k, lambda po=po, pob=pob, outp=outp, hj_=hj_: normalize(po, pob, 512, outp, hj_ % 2)))
        while pending and pending[0][0] <= k:
            pending.pop(0)[1]()

    LOOKAHEAD = 3
    pending = []
    for k in range(len(its) + LOOKAHEAD):
        if k < len(its):
            emit_qk(k)
        if k - LOOKAHEAD >= 0:
            emit_pv(k - LOOKAHEAD)
    while pending:
        pending.pop(0)[1]()

    if STAGE < 3.5:
        return finish(nc, P, st, O, None, None, None, None)

    P.epoch()
    top[0] = slot0 + 8 * 1544 * 2 + 64
    kraw = [alloc((512,), F32) for _ in range(4)]
    krb = [Buf() for _ in range(4)]
    vraw = [alloc((512,), F32) for _ in range(4)]
    vrb = [Buf() for _ in range(4)]
    kTc = alloc((4, PAST), BF16)
    kTcb = [Buf() for _ in range(8)]
    Vc2 = [alloc((8, NH, 65), BF16) for _ in range(2)]
    Vcb2 = [[Buf() for _ in range(8)] for _ in range(2)]
    spost = alloc((4, 8, NH), F32)
    Ecache = alloc((4, 8, NH), F32)
    Enew = alloc((NH,), F32)
    en32 = alloc((NH, 64), F32)
    PTn = alloc((NH, 64), BF16)
    PTs = alloc((8, NH, 16), BF16)
    PTsb = [Buf() for _ in range(8)]
    sb2 = Buf("sample2")
    assert top[0] <= P2S
    MSET("vector", spost[:, :, 7, :], 0.0, [sb2])
    for c in range(6, -1, -1):
        TT("vector", spost[:, :, c, :], spost[:, :, c + 1, :], clf_sb[:, :, c + 1, :], ALU.add, [sb2, clfb], [sb2])
    pt_, ptb = PSI(0)
    MM(pt_[:, 0:256], triLs, clf_sb.rearrange("p b c h -> p (b c h)"), True, False, [sb2, clfb, cst], [ptb], inc=False)
    MM(pt_[:, 0:256], ones, spost.rearrange("p b c h -> p (b c h)"), False, True, [sb2, cst], [ptb])
    ACT(Ecache.rearrange("p b c h -> p (b c h)"), pt_[:, 0:256], AF.Exp, [ptb], [sb2])
    pcq, pcqb = PSI(1)
    MM(pcq[0:64, 0:8], triUb[0:64, 0:64], logf_all[0:64, 16, :], True, True, [lb, cst], [pcqb])
    ACT(Enew[0:64], pcq[0:64, 0:8], AF.Exp, [pcqb], [sb2], scale=-1.0)
    TT("vector", Vnew[0:64], Vnew[0:64], Enew[0:64].unsqueeze(2).to_broadcast([64, NH, 65]), ALU.mult,
       [sab, sb2], [sab])
    if SUB == 20:
        return finish(nc, P, st, O, None, None, None, None)
    psn2 = [PSI(2), PSI(1)]
    en4 = en32.rearrange("p (a two) q -> p a two q", two=2)
    for par in range(2):
        psn, psnb = psn2[par]
        pp = 64 * par
        for a in range(4):
            MM(psn[0:64, a * 64:(a + 1) * 64], kTs[pp:pp + 64, a, :], qTs[pp:pp + 64, a, :], True, True,
               [sab], [psnb], inc=(a == 3))
        ACT(en4[0:64, :, par, :], psn[0:64, 0:256].rearrange("p (a q) -> p a q", a=4), AF.Exp, [psnb], [sb2])
    TT("vector", PTn[0:64], en32[0:64], masks[0:64].unsqueeze(1).to_broadcast([64, NH, 64]), ALU.mult, [sb2, cst],
       [sb2])
    if SUB == 21:
        return finish(nc, P, st, O, None, None, None, None)
    pos, posb = PSI(3)
    ck_v = I["ck"]
    cv_v = I["cv"]
    def p3b_T(k):
        b, c = divmod(k, 8)
        s_ = k % 4
        Vc, Vcb = Vc2[b % 2], Vcb2[b % 2]
        DMA("sync", kraw[s_], ck_v[b, c * 128:(c + 1) * 128, :], [], [krb[s_]], f"kraw{s_}")
        DMA("sync", vraw[s_], cv_v[b, c * 128:(c + 1) * 128, :], [], [vrb[s_]], f"vraw{s_}")
        ptr, ptrb = PSI(4 + k % 2)
        for pr in range(4):
            TR(ptr[:, pr * 128:(pr + 1) * 128], kraw[s_][:, pr * 128:(pr + 1) * 128], ident, [krb[s_], cst],
               [ptrb], inc=(pr == 3))
        CP("vector", kTc[:, :, c * 128:(c + 1) * 128], ptr[:, :].rearrange("p (a t) -> p a t", a=4), [ptrb],
           [kTcb[c]])
        TT("vector", Vc[:, c, :, 0:64], vraw[s_].rearrange("p (h d) -> p h d", h=NH),
           Ecache[:, b, c, :].unsqueeze(2).to_broadcast([128, NH, HD]), ALU.mult, [vrb[s_], sb2], [Vcb[c]])
        CP("vector", Vc[:, c, :, 64:65], Ecache[:, b, c, :].unsqueeze(2), [sb2], [Vcb[c]])

    def p3b_S(k):
        b, c = divmod(k, 8)
        PT4 = PTs[:, c].rearrange("p (a two) q -> p a two q", two=2)
        for par in range(2):
            pss, pssb = PSI(6 + par)
            pp = 64 * par
            for a in range(4):
                MM(pss[:, a * 16:(a + 1) * 16], kTc[pp:pp + 64, a, c * 128:(c + 1) * 128],
                   qTs[pp:pp + 64, a, b * 16:(b + 1) * 16], True, True, [kTcb[c], sab], [pssb], inc=(a == 3))
            ACT(PT4[:, :, par, :], pss[:, 0:64].rearrange("p (a q) -> p a q", a=4), AF.Exp, [pssb], [PTsb[c]])

    def p3b_PV(b):
        nonlocal_first = first_flag
        Vc, Vcb = Vc2[b % 2], Vcb2[b % 2]
        for h in range(NH):
            o = pos[0:65, (b * 8 + h) * 16:(b * 8 + h) * 16 + 16]
            for c in range(8):
                MM(o, Vc[:, c, h, :], PTs[:, c, h, :], nonlocal_first[0], False, [Vcb[c], PTsb[c]], [posb], inc=False)
                nonlocal_first[0] = False
            MM(o, Vnew[0:64, h, :], PTn[0:64, h, b * 16:(b + 1) * 16], False, True, [sab, sb2], [posb])

    first_flag = [True]
    p3b_T(0)
    for k in range(32):
        if k + 1 < 32:
            p3b_T(k + 1)
        p3b_S(k)
        if k % 8 == 7:
            p3b_PV(k // 8)

    def outs(R_, Rb_):
        for h in range(NH):
            pp = 64 * (h % 2)
            TT("vector", catA[pp:pp + 64, h // 2, S:S + 64].rearrange("p (b q) -> p b q", b=4),
               pos[0:64, :].rearrange("p (b h q) -> p b h q", b=4, h=NH)[:, :, h, :],
               R_[0:64, :].rearrange("p (b h q) -> p b h q", b=4, h=NH)[:, :, h, :], ALU.mult, [posb, Rb_],
               [catAb[h // 2][4]])
    if SUB in (22, 23, 24):
        return finish(nc, P, st, O, None, None, None, None)
    normalize(pos, posb, 512, outs, 0)

    if STAGE < 4:
        return finish(nc, P, st, O, None, None, None, None)

    P.epoch()
    top[0] = slot0 + 8 * 1544 * 2 + 64
    G1 = [alloc((D,), F32) for _ in range(2)]
    G1b = Buf()
    xin = [alloc((D,), F32) for _ in range(3)]
    xinb = [Buf() for _ in range(3)]
    x1t = [alloc((D,), F32) for _ in range(3)]
    x1b = [Buf() for _ in range(3)]
    xn4 = [alloc((D,), F32) for _ in range(3)]
    junk4 = alloc((D,), BF16)
    stat4 = alloc((17, 4), F32)
    assert top[0] <= P2S
    nb.update(xt=x1t, xtb=x1b, xn=xn4, xnb=[Buf(), Buf(), Buf()], junk=junk4, junkb=Buf(), stat=stat4,
              statb=[Buf() for _ in range(17)])

    def gate_tiles(G, Gb, col0):
        for gi, (sel, M) in enumerate(((sel_p, 128), (sel_s, 64))):
            for half in range(2):
                ps, pb = PS()
                MM(ps[0:M, :], sel[0:5, 0:M], modTM[0:5, col0 + half * 512:col0 + (half + 1) * 512], True, True,
                   [modtb, cst], [pb])
                CP("vector", G[gi][0:M, half * 512:(half + 1) * 512], ps[0:M, :], [pb], [Gb])

    gate_tiles(G1, G1b, 0)
    h2b = [[Buf() for _ in range(8)] for _ in range(17)]
    x1s_b = [Buf() for _ in range(17)]
    for i in range(17):
        R, tok0 = tile_rows(i)
        slot = i % 3
        ci = i // 4 if i < 16 else 4
        src = xp_v[tok0:tok0 + R, :] if i < 16 else I["xs"]
        DMA("sync", xin[slot][0:R], src, [], [xinb[slot]], f"xin{slot}")
        Gt = G1[0] if i < 16 else G1[1]
        for half in range(2):
            ps, pb = PS()
            for kc in range(8):
                lhs = catC[:, kc, tok0:tok0 + R] if kc < 4 else catA[:, kc - 4, tok0:tok0 + R]
                rb = catb[kc][ci] if kc < 4 else catAb[kc - 4][ci]
                MM(ps[0:R, :], lhs, w_o[:, kc, half * 512:(half + 1) * 512], kc == 0, kc == 7, [rb, wob], [pb])
            TT("vector", x1t[slot][0:R, half * 512:(half + 1) * 512], ps[0:R, :], Gt[0:R, half * 512:(half + 1) * 512],
               ALU.mult, [pb, G1b], [x1b[slot]])
        TT("vector", x1t[slot][0:R], x1t[slot][0:R], xin[slot][0:R], ALU.add, [xinb[slot]], [x1b[slot]])
        DMA("gpsimd", x1s[tok0:tok0 + R, :], x1t[slot][0:R], [x1b[slot]], [x1s_b[i]], f"x1spill{slot}")
        norm_A(i, slot)
        if i > 1:
            norm_B(i - 2, A2, B2, hT, h2b)
    norm_B(15, A2, B2, hT, h2b)
    norm_B(16, A2, B2, hT, h2b)

    if STAGE < 5:
        return finish(nc, P, st, O, None, None, None, None)

    P.epoch()
    top[0] = slot0
    hid = alloc((NF, TT_), BF16)
    hidb = [[Buf() for _ in range(5)] for _ in range(NF)]
    wgu_off = top[0]
    wg = [alloc((8, 256), BF16) for _ in range(2)]
    wu = [alloc((8, 256), BF16) for _ in range(2)]
    wgb = [Buf() for _ in range(2)]
    wub2 = [Buf() for _ in range(2)]
    sil = [alloc((512,), F32) for _ in range(2)]
    silb = [Buf() for _ in range(2)]
    wd0 = alloc((NF, 512), BF16)
    wdb = [Buf(), Buf()]
    print("P5 top", top[0])
    w_gate_v = I["w_gate"].rearrange("(k p) n -> p k n", p=128)
    w_up_v = I["w_up"].rearrange("(k p) n -> p k n", p=128)
    w_down_v = I["w_down"].rearrange("(f p) n -> p f n", p=128)
    chunks = [(512 * j, 512, [4 * j + t for t in range(4)]) for j in range(4)] + [(S, 64, [16])]
    sc = 0
    for f in range(NF):
        g2, fo = f // 2, f % 2
        s_ = g2 % 2
        if fo == 0:
            DMA("gpsimd", wg[s_], w_gate_v[:, :, g2 * 256:g2 * 256 + 256], [], [wgb[s_]], f"wg{s_}")
            DMA("gpsimd", wu[s_], w_up_v[:, :, g2 * 256:g2 * 256 + 256], [], [wub2[s_]], f"wu{s_}")
        if f == 3:
            for (f0, f1) in ((0, 8), (8, 16), (16, NF)):
                DMA("gpsimd", wd0[:, f0:f1, :], w_down_v[:, f0:f1, 0:512], [], [wdb[0]], "wd0")
        for ci, (t0, n, tiles) in enumerate(chunks):
            pg, pgb = PS()
            pu, pub = PS()
            for (pp, ppb, w_, wb_) in ((pg, pgb, wg[s_], wgb[s_]), (pu, pub, wu[s_], wub2[s_])):
                for kc in range(8):
                    MM(pp[:, 0:n], w_[:, kc, fo * 128:(fo + 1) * 128], hT[:, kc, t0:t0 + n], kc == 0, kc == 7,
                       [wb_] + [h2b[t][kc] for t in tiles], [ppb])
            k_ = sc % 2
            sc += 1
            ACT(sil[k_][:, 0:n], pg[:, 0:n], AF.Silu, [pgb], [silb[k_]])
            TT("vector", hid[:, f, t0:t0 + n], pu[:, 0:n], sil[k_][:, 0:n], ALU.mult, [pub, silb[k_]], [hidb[f][ci]])

    P.epoch()
    top[0] = wgu_off
    yt = [alloc((512,), F32) for _ in range(2)]
    ytb = [Buf() for _ in range(2)]
    x1r = [alloc((512,), F32) for _ in range(2)]
    x1rb = [Buf() for _ in range(2)]
    print("P5b top", top[0])
    wd1 = hT.rearrange("p a t -> p (a t)")[:, 0:NF * 512].rearrange("p (f n) -> p f n", f=NF)
    wdb[1] = Buf()
    cflat = catC.rearrange("p a t -> p (a t)")
    G2 = [cflat[:, 0:2 * D].bitcast(F32), cflat[:, 2 * D:4 * D].bitcast(F32)]
    G2b = Buf()
    wd = [wd0, wd1]
    for (f0, f1) in ((0, 8), (8, 16), (16, NF)):
        DMA("gpsimd", wd1[:, f0:f1, :], w_down_v[:, f0:f1, 512:1024], [], [wdb[1]], "wd1")
    gate_tiles(G2, G2b, 1024)
    cnt5 = 0
    for half in range(2):
        for i in range(17):
            R, tok0 = tile_rows(i)
            slot = cnt5 % 2
            cnt5 += 1
            ci = i // 4 if i < 16 else 4
            DMA("sync", x1r[slot][0:R], x1s[tok0:tok0 + R, half * 512:(half + 1) * 512], [x1s_b[i]], [x1rb[slot]],
                f"x1r{slot}")
            Gt = G2[0] if i < 16 else G2[1]
            ps, pb = PS()
            for f in range(NF):
                MM(ps[0:R, :], hid[:, f, tok0:tok0 + R], wd[half][:, f, :], f == 0, f == NF - 1,
                   [hidb[f][ci], wdb[half]], [pb])
            TT("vector", yt[slot][0:R], ps[0:R, :], Gt[0:R, half * 512:(half + 1) * 512], ALU.mult, [pb, G2b],
               [ytb[slot]])
            TT("vector", yt[slot][0:R], yt[slot][0:R], x1r[slot][0:R], ALU.add, [x1rb[slot]], [ytb[slot]])
            dst = O["yp"][tok0:tok0 + R, half * 512:(half + 1) * 512] if i < 16 else O["ys"][:, half * 512:(half + 1) * 512]
            DMA("sync", dst, yt[slot][0:R], [ytb[slot]], [Buf()], f"yout{slot}")
    return finish(nc, P, st, O, None, None, None, None)


def finish(nc, P, st, O, zero_out, alloc, MSET, DMA):
    P.barrier()
    print("MARKS", [(m["tensor"], m["scalar"], m["vector"], m["gpsimd"]) for m in P.marks])
    P.emit(nc)
    st.close()
    return nc


_CACHE = {}


def _consts():
    c = {}
    c["ident"] = np.eye(128, dtype=np.float32)
    r = np.arange(128)
    c["triU"] = (r[:, None] <= r[None, :]).astype(np.float32)
    c["ones"] = np.ones((128, 128), np.float32)
    c["triLs"] = (r[:, None] > r[None, :]).astype(np.float32)
    r64 = np.arange(64)
    same = (r64[:, None] // 16) == (r64[None, :] // 16)
    c["triUb"] = (same & (r64[:, None] <= r64[None, :])).astype(np.float32)
    c["masks"] = (same & (r64[:, None] <= r64[None, :])).astype(np.float32)
    md = np.zeros((128, 4, 512), np.float32)
    q = np.arange(512)
    for rr in range(4):
        md[:, rr, :] = np.where(128 * rr + r[:, None] > q[None, :], -1e30, 0.0)
    c["maskd"] = md
    sp = np.zeros((5, 128), np.float32)
    sp[0, :] = 1.0
    ss = np.zeros((5, 128), np.float32)
    for m in range(64):
        ss[1 + m // 16, m] = 1.0
    c["sel_p"] = sp
    c["sel_s"] = ss
    return c


def kernel(x_prompt, x_sample, cache_k, cache_v, cache_logf, state_conv, c_prompt, c_sample,
           w_ada, b_ada, norm1_g, w_in, b_f, q_norm_g, k_norm_g, conv_w, conv_b, conv_ln_g,
           conv_ln_b, w_out, norm2_g, w_gate, w_up, w_down):
    f = lambda a: np.ascontiguousarray(np.asarray(a, dtype=np.float32))
    x_prompt, x_sample, cache_k, cache_v, cache_logf, state_conv = map(f, (x_prompt, x_sample, cache_k, cache_v,
                                                                           cache_logf, state_conv))
    c_prompt, c_sample = f(c_prompt), f(c_sample)
    if "nc" not in _CACHE:
        _CACHE["nc"] = build_program()
    nc = _CACHE["nc"]
    cst = _consts()

    def fm(v, nch):
        return f(np.asarray(v).reshape(nch, 128).T)

    shared = dict(
        w_ada=f(w_ada[0]), b_adaT=fm(b_ada[0], 48),
        bada_g5=f(np.tile(np.concatenate([b_ada[0][2048:3072], b_ada[0][5120:6144]])[None, :], (5, 1))),
        n1gT=fm(norm1_g[0], 8), n2gT=fm(norm2_g[0], 8), w_in=f(w_in[0]),
        bf_bc=f(np.tile(np.asarray(b_f[0])[None, :], (128, 1))),
        gq_bc=f(np.tile(np.asarray(q_norm_g[0]).reshape(1, 512), (128, 1))),
        gk_bc=f(np.tile(np.asarray(k_norm_g[0]).reshape(1, 512), (128, 1))),
        conv_wT=f(np.asarray(conv_w[0]).T.reshape(4, 128, CW).transpose(1, 0, 2)),
        conv_bT=fm(conv_b[0], 4), ln_gT=fm(conv_ln_g[0], 4), ln_bT=fm(conv_ln_b[0], 4),
        w_out=f(w_out[0]), w_gate=f(w_gate[0]), w_up=f(w_up[0]), w_down=f(w_down[0]), **cst)
    in_maps = []
    for i in range(8):
        sb = slice(4 * i, 4 * i + 4)
        crow = np.concatenate([c_prompt[i:i + 1], c_sample[sb]], 0)
        m = dict(shared)
        m.update(
            xp=x_prompt[i], xs=f(x_sample[sb].reshape(64, D)),
            ck=f(cache_k[0, sb].reshape(4, PAST, 512)), cv=f(cache_v[0, sb].reshape(4, PAST, 512)),
            clf=f(cache_logf[0, sb]), sconv=f(state_conv[0, sb]),
            sconvT=f(state_conv[0, sb].transpose(2, 0, 1).reshape(4, 128, 4, 30).transpose(1, 0, 2, 3)),
            cT=f(crow.T.reshape(8, 128, 5).transpose(1, 0, 2)),
        )
        in_maps.append(m)
    res = run_bass_kernel_spmd(nc, in_maps, core_ids=list(range(8)))
    R = res.results
    g = lambda k: np.stack([np.asarray(r[k], dtype=np.float32) for r in R], 0)
    yp = g("yp")
    ys = g("ys").reshape(32, 16, D)
    kp = g("kp").reshape(1, 8, S, NH, HD)
    vp = g("vp").reshape(1, 8, S, NH, HD)
    lfp = g("lfp").reshape(1, 8, S, NH)
    cvp = g("cvp").reshape(1, 8, 30, DC)
    ksm = g("ksm").reshape(1, 32, 16, NH, HD)
    vsm = g("vsm").reshape(1, 32, 16, NH, HD)
    lfs = g("lfs").reshape(1, 32, 16, NH)
    cvs = g("cvs").reshape(1, 32, 30, DC)
    return (yp, ys, kp, vp, lfp, cvp, ksm, vsm, lfs, cvs)
```
